# Optimizing a Trainium2 kernel written in Bass

```python
import math
import jax, jax.numpy as jnp
from jax import lax
import numpy as np

D_MODEL = 1024
BATCH = 4
SEQ = 8192
DEPTH = 1

N_META = 16
MIX_WIDTH = D_MODEL
ATTN_WIDTH = D_MODEL // 2
SSM_WIDTH = MIX_WIDTH - ATTN_WIDTH
N_HEADS = 8
V_HEAD_DIM = ATTN_WIDTH // N_HEADS
QK_NOPE_DIM = 64
QK_ROPE_DIM = 32
QK_HEAD_DIM = QK_NOPE_DIM + QK_ROPE_DIM
Q_LORA_RANK = 384
KV_LORA_RANK = 256
ROPE_BASE = 10000.0
Q_BLOCK = 128
SSM_GROUP = 16
SSM_GROUPS = SSM_WIDTH // SSM_GROUP
SSM_STATE = 64
N_DIR = 2
DT_MIN = 1e-3
DT_MAX = 1e-1
D_FF = 4 * D_MODEL
EPS = 1e-6
OFF_KV = Q_LORA_RANK
OFF_KR = OFF_KV + KV_LORA_RANK
OFF_U = OFF_KR + QK_ROPE_DIM
IN_WIDTH = OFF_U + SSM_WIDTH

kernel_name = "hymba_mla_s5_sandwich_encoder_layer"


def rms_norm(x, g):
    x32 = x.astype(jnp.float32)
    y = x32 * lax.rsqrt(jnp.mean(x32 * x32, axis=-1, keepdims=True) + EPS)
    return (y * g.astype(jnp.float32)).astype(x.dtype)


def rope_tables(pos):
    inv = 1.0 / (ROPE_BASE ** (jnp.arange(0, QK_ROPE_DIM, 2, dtype=jnp.float32) / QK_ROPE_DIM))
    ang = pos.astype(jnp.float32)[..., None] * inv
    return jnp.cos(ang), jnp.sin(ang)


def apply_rope(x, cos, sin):
    x1, x2 = jnp.split(x.astype(jnp.float32), 2, axis=-1)
    return jnp.concatenate([x1 * cos - x2 * sin, x1 * sin + x2 * cos], axis=-1).astype(x.dtype)


def mla_bidirectional(c_q, c_kv, k_r, cos, sin, g_q_lat, w_uq, g_kv_lat, w_ukv):
    bsz, L, _ = c_q.shape
    q = (rms_norm(c_q, g_q_lat) @ w_uq).reshape(bsz, L, N_HEADS, QK_HEAD_DIM)
    q_nope = q[..., :QK_NOPE_DIM]
    q_rope = apply_rope(q[..., QK_NOPE_DIM:], cos[:, :, None, :], sin[:, :, None, :])
    kv = (rms_norm(c_kv, g_kv_lat) @ w_ukv).reshape(bsz, L, N_HEADS, QK_NOPE_DIM + V_HEAD_DIM)
    k_nope = kv[..., :QK_NOPE_DIM]
    v = kv[..., QK_NOPE_DIM:]
    k_rope = apply_rope(k_r, cos, sin)
    scale = QK_HEAD_DIM ** -0.5
    n_blk = -(-L // Q_BLOCK)
    pad = n_blk * Q_BLOCK - L

    def to_blocks(t):
        t = jnp.pad(t, ((0, 0), (0, pad), (0, 0), (0, 0)))
        return t.reshape(bsz, n_blk, Q_BLOCK, N_HEADS, t.shape[-1]).transpose(1, 0, 2, 3, 4)

    def attend(blk):
        qn, qr = blk
        s = (jnp.einsum('bqhd,bkhd->bhqk', qn, k_nope, preferred_element_type=jnp.float32)
             + jnp.einsum('bqhr,bkr->bhqk', qr, k_rope, preferred_element_type=jnp.float32))
        p = jax.nn.softmax(s * scale, axis=-1)
        return jnp.einsum('bhqk,bkhd->bqhd', p.astype(v.dtype), v)

    o = lax.map(attend, (to_blocks(q_nope), to_blocks(q_rope)))
    o = o.transpose(1, 0, 2, 3, 4).reshape(bsz, n_blk * Q_BLOCK, N_HEADS * V_HEAD_DIM)
    return o[:, :L]


def _ssm_combine(left, right):
    a_l, b_l = left
    a_r, b_r = right
    return a_r * a_l, a_r * b_l + b_r


def s5_bidirectional(u, A_re, A_im, log_dt, B_re, B_im, C_re, C_im, D):
    bsz, L, _ = u.shape
    u32 = u.astype(jnp.float32).reshape(bsz, L, SSM_GROUPS, SSM_GROUP)
    uc = u32.astype(jnp.complex64)
    y = D.astype(jnp.float32).reshape(SSM_GROUPS, SSM_GROUP) * u32
    for d in range(N_DIR):
        lam = lax.complex(jnp.minimum(A_re[d].astype(jnp.float32), -1e-4), A_im[d].astype(jnp.float32))
        dt = jnp.exp(log_dt[d].astype(jnp.float32))[:, None]
        lam_bar = jnp.exp(lam * dt)
        b_c = lax.complex(B_re[d].astype(jnp.float32), B_im[d].astype(jnp.float32))
        b_bar = ((lam_bar - 1.0) / lam)[..., None] * b_c
        bu = jnp.einsum('blgh,gph->blgp', uc, b_bar)
        a = jnp.broadcast_to(lam_bar, bu.shape)
        _, xs = lax.associative_scan(_ssm_combine, (a, bu), axis=1, reverse=(d == 1))
        c_c = lax.complex(C_re[d].astype(jnp.float32), C_im[d].astype(jnp.float32))
        y = y + jnp.real(jnp.einsum('ghp,blgp->blgh', c_c, xs))
    return y.reshape(bsz, L, SSM_WIDTH)


def setup_inputs(seed: int = 0) -> dict:
    key = jax.random.key(seed)
    ks = jax.random.split(key, 32)
    nrm = lambda k, shape, s: jax.random.normal(k, shape, jnp.float32) * s
    gain = lambda k, shape: 1.0 + 0.02 * jax.random.normal(k, shape, jnp.float32)
    x = jax.random.normal(ks[0], (BATCH, SEQ, D_MODEL), jnp.float32)
    positions = jnp.broadcast_to(jnp.arange(SEQ, dtype=jnp.int32), (BATCH, SEQ))
    meta_tokens = nrm(ks[1], (N_META, D_MODEL), 1.0)
    n_idx = jnp.arange(SSM_STATE, dtype=jnp.float32)
    ssm_A_re = -0.5 + 0.01 * jax.random.normal(ks[8], (DEPTH, N_DIR, SSM_GROUPS, SSM_STATE), jnp.float32)
    ssm_A_im = jnp.broadcast_to(math.pi * n_idx, (DEPTH, N_DIR, SSM_GROUPS, SSM_STATE)) \
        + 0.01 * jax.random.normal(ks[9], (DEPTH, N_DIR, SSM_GROUPS, SSM_STATE), jnp.float32)
    ssm_log_dt = jax.random.uniform(ks[10], (DEPTH, N_DIR, SSM_GROUPS), jnp.float32,
                                    math.log(DT_MIN), math.log(DT_MAX))
    return {
        "x": x,
        "positions": positions,
        "meta_tokens": meta_tokens,
        "g_pre_mix": gain(ks[2], (DEPTH, D_MODEL)),
        "w_in": nrm(ks[3], (DEPTH, D_MODEL, IN_WIDTH), D_MODEL ** -0.5),
        "g_q_lat": gain(ks[4], (DEPTH, Q_LORA_RANK)),
        "w_uq": nrm(ks[5], (DEPTH, Q_LORA_RANK, N_HEADS * QK_HEAD_DIM), Q_LORA_RANK ** -0.5),
        "g_kv_lat": gain(ks[6], (DEPTH, KV_LORA_RANK)),
        "w_ukv": nrm(ks[7], (DEPTH, KV_LORA_RANK, N_HEADS * (QK_NOPE_DIM + V_HEAD_DIM)), KV_LORA_RANK ** -0.5),
        "ssm_A_re": ssm_A_re,
        "ssm_A_im": ssm_A_im,
        "ssm_log_dt": ssm_log_dt,
        "ssm_B_re": nrm(ks[11], (DEPTH, N_DIR, SSM_GROUPS, SSM_STATE, SSM_GROUP), (2 * SSM_GROUP) ** -0.5),
        "ssm_B_im": nrm(ks[12], (DEPTH, N_DIR, SSM_GROUPS, SSM_STATE, SSM_GROUP), (2 * SSM_GROUP) ** -0.5),
        "ssm_C_re": nrm(ks[13], (DEPTH, N_DIR, SSM_GROUPS, SSM_GROUP, SSM_STATE), SSM_STATE ** -0.5),
        "ssm_C_im": nrm(ks[14], (DEPTH, N_DIR, SSM_GROUPS, SSM_GROUP, SSM_STATE), SSM_STATE ** -0.5),
        "ssm_D": nrm(ks[15], (DEPTH, SSM_WIDTH), 1.0),
        "w_glu": nrm(ks[16], (DEPTH, SSM_WIDTH, 2 * SSM_WIDTH), SSM_WIDTH ** -0.5),
        "g_mix_out": gain(ks[17], (DEPTH, MIX_WIDTH)),
        "w_out": nrm(ks[18], (DEPTH, MIX_WIDTH, D_MODEL), MIX_WIDTH ** -0.5),
        "g_post_mix": gain(ks[19], (DEPTH, D_MODEL)),
        "g_pre_mlp": gain(ks[20], (DEPTH, D_MODEL)),
        "w_mlp_up": nrm(ks[21], (DEPTH, D_MODEL, D_FF), D_MODEL ** -0.5),
        "w_mlp_down": nrm(ks[22], (DEPTH, D_FF, D_MODEL), D_FF ** -0.5),
        "g_post_mlp": gain(ks[23], (DEPTH, D_MODEL)),
    }


def reference(x, positions, meta_tokens, g_pre_mix, w_in, g_q_lat, w_uq, g_kv_lat, w_ukv,
              ssm_A_re, ssm_A_im, ssm_log_dt, ssm_B_re, ssm_B_im, ssm_C_re, ssm_C_im, ssm_D,
              w_glu, g_mix_out, w_out, g_post_mix, g_pre_mlp, w_mlp_up, w_mlp_down, g_post_mlp):
    bsz = x.shape[0]
    meta = jnp.broadcast_to(meta_tokens.astype(x.dtype)[None], (bsz, N_META, x.shape[-1]))
    h = jnp.concatenate([meta, x], axis=1)
    meta_pos = jnp.broadcast_to(jnp.arange(N_META, dtype=jnp.int32), (bsz, N_META))
    pos = jnp.concatenate([meta_pos, positions.astype(jnp.int32) + N_META], axis=1)
    cos, sin = rope_tables(pos)
    for i in range(DEPTH):
        xn = rms_norm(h, g_pre_mix[i])
        proj = xn @ w_in[i]
        c_q = proj[..., :OFF_KV]
        c_kv = proj[..., OFF_KV:OFF_KR]
        k_r = proj[..., OFF_KR:OFF_U]
        u = proj[..., OFF_U:]
        attn = mla_bidirectional(c_q, c_kv, k_r, cos, sin, g_q_lat[i], w_uq[i], g_kv_lat[i], w_ukv[i])
        y = s5_bidirectional(u, ssm_A_re[i], ssm_A_im[i], ssm_log_dt[i], ssm_B_re[i], ssm_B_im[i],
                             ssm_C_re[i], ssm_C_im[i], ssm_D[i])
        z = jax.nn.gelu(y) @ w_glu[i].astype(jnp.float32)
        ssm = (z[..., :SSM_WIDTH] * jax.nn.sigmoid(z[..., SSM_WIDTH:])).astype(h.dtype)
        g_mix = g_mix_out[i]
        mix = jnp.concatenate([rms_norm(attn, g_mix[:ATTN_WIDTH]),
                               rms_norm(ssm, g_mix[ATTN_WIDTH:])], axis=-1)
        h = h + rms_norm(mix @ w_out[i], g_post_mix[i])
        m = rms_norm(h, g_pre_mlp[i]) @ w_mlp_up[i]
        m = jnp.square(jax.nn.relu(m)) @ w_mlp_down[i]
        h = h + rms_norm(m, g_post_mlp[i])
    return h[:, N_META:]
```

```python
import contextlib
import math
import numpy as np
import concourse.bass as bass
import concourse.mybir as mybir
from concourse.bass_utils import run_bass_kernel_spmd

F32 = mybir.dt.float32
BF16 = mybir.dt.bfloat16
I32 = mybir.dt.int32
ALU = mybir.AluOpType
AF = mybir.ActivationFunctionType

ENGS = ("pe", "act", "dve", "pool", "sp")
EPOCH = 12000
NDMASEM = 32
import os
NOSELF = bool(os.environ.get('NOSELF'))
DMASEM_RANGE = {"sp": (0, 16), "pool": (16, 8), "act": (24, 8)}

D_MODEL = 1024
SEQ = 8192
NMETA = 16
HALF = 4096
NT = 65
NTOK = NT * 128
NKEY = SEQ + NMETA
EPS = 1e-6
SCALE = 96 ** -0.5
J = 1026
SEGL = 26
NL = 1040
NS = 520
TWO_PI = 2.0 * math.pi
MAGIC = 12582912.0


class Prog:
    def __init__(self, nc):
        self.nc = nc
        self.ops = []
        self.last_w = {}
        self.readers = {}
        self.eng_ops = {e: [] for e in ENGS}
        self.ndma = 0
        self.ndma_eng = {}
        self.pending_barrier = {}
        self.bank_last = {}
        self.dma_since_barrier = []
        self.deferred = []
        self.dma_last_on_sem = {}

    def op(self, eng, fn, reads=(), writes=(), dma=False, aps=(), banks=None, defer=False):
        oid = len(self.ops)
        deps = set()
        if banks is None:
            banks = set()
            for a in aps:
                nm = getattr(a, "name", "")
                if isinstance(nm, str) and nm.startswith("ps_"):
                    banks.add(nm)
        for b in banks:
            la = self.bank_last.setdefault(b, {})
            for e2, o2 in la.items():
                if e2 != eng:
                    deps.add(o2)
            la[eng] = oid
        for k in reads:
            w = self.last_w.get(k)
            if w is not None:
                deps.add(w)
        for k in writes:
            w = self.last_w.get(k)
            if w is not None:
                deps.add(w)
            for r in self.readers.get(k, {}).values():
                deps.add(r)
        if eng in self.pending_barrier:
            deps |= self.pending_barrier.pop(eng)
        rec = dict(id=oid, eng=eng, fn=fn, deps=deps, dma=dma, needed=False)
        if dma:
            (self.deferred if defer else self.dma_since_barrier).append(oid)
            base, n = DMASEM_RANGE[eng]
            cnt = self.ndma_eng.get(eng, 0)
            k = base + cnt % n
            rec["dsem"] = k
            rec["dval"] = 16 * (cnt // n + 1)
            prev = self.dma_last_on_sem.get(k)
            if prev is not None:
                deps.add(prev)
            self.dma_last_on_sem[k] = oid
            self.ndma_eng[eng] = cnt + 1
            self.ndma += 1
        deps.discard(oid)
        self.ops.append(rec)
        self.eng_ops[eng].append(oid)
        for k in writes:
            self.last_w[k] = oid
            self.readers[k] = {}
        for k in reads:
            self.readers.setdefault(k, {})[("dma", oid) if dma else eng] = oid
        return oid

    def barrier(self, include_deferred=True):
        allp = set(self.dma_since_barrier)
        if include_deferred:
            allp |= set(self.deferred)
            self.deferred = []
        for e in ENGS:
            for oid in reversed(self.eng_ops[e]):
                if not self.ops[oid]["dma"]:
                    allp.add(oid)
                    break
        for e in ENGS:
            self.pending_barrier[e] = set(allp) | self.pending_barrier.get(e, set())
        self.dma_since_barrier = []

    def emit(self):
        nc = self.nc
        ops = self.ops
        for rec in ops:
            eng_max = {}
            dma_deps = []
            for d in rec["deps"]:
                p = ops[d]
                if p["dma"]:
                    dma_deps.append(d)
                else:
                    if p["eng"] == "pe" and rec["eng"] == "pe" and not rec["dma"]:
                        continue
                    if NOSELF and p["eng"] == rec["eng"] and not rec["dma"]:
                        continue
                    eng_max[p["eng"]] = max(eng_max.get(p["eng"], -1), d)
            rec["w_eng"] = eng_max
            rec["w_dma"] = dma_deps
        for e in ENGS:
            seen = {}
            seen_dma = set()
            for oid in self.eng_ops[e]:
                rec = ops[oid]
                ne = {}
                for pe_, d in rec["w_eng"].items():
                    if seen.get(pe_, -1) >= d:
                        continue
                    seen[pe_] = d
                    ne[pe_] = d
                rec["w_eng"] = ne
                nd = []
                for d in rec["w_dma"]:
                    if d in seen_dma:
                        continue
                    seen_dma.add(d)
                    nd.append(d)
                rec["w_dma"] = nd
        for rec in ops:
            for d in rec["w_eng"].values():
                ops[d]["needed"] = True
        nsem = {}
        for e in ENGS:
            c = 0
            for oid in self.eng_ops[e]:
                rec = ops[oid]
                if rec["needed"] and not rec["dma"]:
                    rec["spos"] = c
                    c += 1
            nsem[e] = (c + EPOCH - 1) // EPOCH
        with contextlib.ExitStack() as st:
            sems = {e: [st.enter_context(nc.semaphore(f"s_{e}_{i}")) for i in range(max(nsem[e], 1))]
                    for e in ENGS}
            dsems = [st.enter_context(nc.semaphore(f"s_dma_{i}")) for i in range(NDMASEM)]
            block = st.enter_context(nc.Block())
            hw = {"pe": block.tensor, "act": block.scalar, "dve": block.vector,
                  "pool": block.gpsimd, "sp": block.sync}

            def run_engine(e):
                def body(engine):
                    for oid in self.eng_ops[e]:
                        rec = ops[oid]
                        for pe_, d in rec["w_eng"].items():
                            sp = ops[d]["spos"]
                            engine.wait_ge(sems[pe_][sp // EPOCH], sp % EPOCH + 1)
                        for d in rec["w_dma"]:
                            engine.wait_ge(dsems[ops[d]["dsem"]], ops[d]["dval"])
                        ins = rec["fn"](engine)
                        if rec["dma"]:
                            ins.then_inc(dsems[rec["dsem"]], 16)
                        elif rec["needed"]:
                            sp = rec["spos"]
                            ins.then_inc(sems[e][sp // EPOCH], 1)
                    if e == "sp":
                        for k, oid in self.dma_last_on_sem.items():
                            engine.wait_ge(dsems[k], ops[oid]["dval"])
                hw[e](body)

            for e in ENGS:
                run_engine(e)


class KB:
    def __init__(self, nc):
        self.nc = nc
        self.P = Prog(nc)
        self.rr = 0

    def dma(self, out, in_, reads=(), writes=(), eng="sp", defer=False):
        self.P.op(eng, lambda e, o=out, i=in_: e.dma_start(out=o, in_=i), reads=reads, writes=writes, dma=True,
                  defer=defer)

    def mm(self, out, lhsT, rhs, start=True, stop=True, reads=(), writes=(), skip=False, banks=None):
        self.P.op("pe", lambda e, o=out, l=lhsT, r=rhs, s=start, t=stop, k=skip:
                  e.matmul(o, l, r, start=s, stop=t, skip_group_check=k), reads=reads, writes=writes,
                  aps=[out], banks=banks)

    def tr(self, out, in_, ident, reads=(), writes=(), banks=None):
        self.P.op("pe", lambda e, o=out, i=in_, d=ident: e.transpose(o, i, d), reads=reads, writes=writes,
                  aps=[out], banks=banks)

    def act(self, out, in_, func, reads=(), writes=(), scale=1.0, bias=None, accum=None, banks=None):
        def f(e, o=out, i=in_, fu=func, s=scale, b=bias, a=accum):
            kw = {}
            if b is not None:
                kw["bias"] = b
            if a is not None:
                kw["accum_out"] = a
            return e.activation(out=o, in_=i, func=fu, scale=s, **kw)
        self.P.op("act", f, reads=reads, writes=writes, aps=[out, in_], banks=banks)

    def ts(self, out, in0, s1, s2, op0, op1=None, reads=(), writes=(), eng="dve", banks=None):
        def f(e, o=out, i=in0, a=s1, b=s2, p0=op0, p1=op1):
            if p1 is None:
                return e.tensor_scalar(out=o, in0=i, scalar1=a, scalar2=None, op0=p0)
            return e.tensor_scalar(out=o, in0=i, scalar1=a, scalar2=b, op0=p0, op1=p1)
        self.P.op(eng, f, reads=reads, writes=writes, aps=[out, in0], banks=banks)

    def tt(self, out, in0, in1, op, reads=(), writes=(), eng="dve", banks=None):
        self.P.op(eng, lambda e, o=out, a=in0, b=in1, p=op: e.tensor_tensor(out=o, in0=a, in1=b, op=p),
                  reads=reads, writes=writes, aps=[out, in0, in1], banks=banks)

    def stt(self, out, in0, scalar, in1, op0, op1, reads=(), writes=(), banks=None):
        self.P.op("dve", lambda e, o=out, a=in0, s=scalar, b=in1, p0=op0, p1=op1:
                  e.scalar_tensor_tensor(out=o, in0=a, scalar=s, in1=b, op0=p0, op1=p1),
                  reads=reads, writes=writes, aps=[out, in0, in1], banks=banks)

    def cp(self, out, in_, reads=(), writes=(), eng="dve", banks=None):
        if eng == "act":
            self.act(out, in_, AF.Copy, reads=reads, writes=writes, banks=banks)
        else:
            self.P.op(eng, lambda e, o=out, i=in_: e.tensor_copy(out=o, in_=i), reads=reads, writes=writes,
                      aps=[out, in_], banks=banks)

    def memset(self, ap, val, writes=(), eng="dve"):
        self.P.op(eng, lambda e, a=ap, v=val: e.memset(a, v), writes=writes)

    def recip(self, out, in_, reads=(), writes=(), banks=None):
        self.P.op("dve", lambda e, o=out, i=in_: e.reciprocal(out=o, in_=i), reads=reads, writes=writes,
                  aps=[out, in_], banks=banks)

    def rstd(self, out, ss, tmp, mhalf, inv_n, reads, writes, tmpkey):
        self.ts(tmp, ss, inv_n, EPS, ALU.mult, ALU.add, reads=reads, writes=[tmpkey])
        self.tt(out, tmp, mhalf, ALU.pow, reads=[tmpkey], writes=writes, eng="pool")


CFG = {}


def build_program(stage="full"):
    nc = bass.Bass("TRN2", target_bir_lowering=False)
    kb = KB(nc)
    P = kb.P
    D = {}

    def inp(name, shape, dt=F32):
        D[name] = nc.dram_tensor(name, list(shape), dt, kind="ExternalInput").ap()

    inp("xm", [NTOK, 1024]); inp("posT", [128, NT], I32); inp("inv16", [128, 16])
    inp("w_in", [1024, 1184]); inp("w_uq", [384, 768]); inp("w_ukv", [256, 1024])
    inp("w_glu", [512, 1024]); inp("w_out", [1024, 1024]); inp("w_up", [1024, 4096]); inp("w_down", [4096, 1024])
    inp("gpre", [128, 8]); inp("gq", [128, 3]); inp("gkv", [128, 2]); inp("gmix", [128, 8]); inp("gpremlp", [128, 8])
    inp("gpost", [128, 1024]); inp("gpostmlp", [128, 1024])
    inp("AreT", [128, 64]); inp("AimT", [128, 64]); inp("ldtB", [128, 64])
    inp("BrT2", [128, 64, 16]); inp("BiT2", [128, 64, 16]); inp("Bmat2", [128, 64, 16])
    inp("CrT2", [128, 64, 16]); inp("CiT2", [128, 64, 16])
    inp("Dvec", [128, 32]); inp("flags", [128, 2]); inp("PW", [128, 4, 2, 8])
    inp("ident", [128, 128]); inp("swap", [128, 128])
    out = nc.dram_tensor("out", [HALF, 1024], F32, kind="ExternalOutput").ap()
    dbg = None
    if stage != "full":
        dbg = nc.dram_tensor("dbg", [128, 8192], F32, kind="ExternalOutput").ap()
    QTs = nc.dram_tensor("QTs", [96, 8, HALF], BF16, kind="Internal").ap()
    Ud = nc.dram_tensor("Ud", [512, 8, J], BF16, kind="Internal").ap()
    Yd = nc.dram_tensor("Yd", [512, 8, 512], BF16, kind="Internal").ap()
    wup_s = nc.dram_tensor("wup_s", [1024, 4096], BF16, kind="Internal").ap()
    wdn_s = nc.dram_tensor("wdn_s", [4096, 1024], BF16, kind="Internal").ap()
    HnT = nc.dram_tensor("HnT", [128, 8, HALF], BF16, kind="Internal").ap()
    H1s = nc.dram_tensor("H1s", [HALF, 1024], F32, kind="Internal").ap()

    with contextlib.ExitStack() as gst:
        def sb(st, name, shape, dt):
            return st.enter_context(nc.sbuf_tensor("sb_" + name, list(shape), dt))

        def ps(st, name, shape, dt):
            return st.enter_context(nc.psum_tensor("ps_" + name, list(shape), dt))

        ident_f = sb(gst, "ident_f", [128, 128], F32)
        ident_b = sb(gst, "ident_b", [128, 128], BF16)
        mhalf = sb(gst, "mhalf", [128, 8], F32)
        gvec = sb(gst, "gvec", [128, 32], F32)
        S1 = gst.enter_context(contextlib.ExitStack())
        attn_tm = sb(S1, "attn_tm", [128, 32, 512], BF16)
        S2 = S1.enter_context(contextlib.ExitStack())
        ckvnT = sb(S2, "ckvnT", [128, 2, NTOK], BF16)
        kropeT = sb(S2, "kropeT", [96, NTOK], BF16)
        kb.dma(ident_f[:], D["ident"], writes=["ident_f"])
        kb.cp(ident_b[:], ident_f[:], reads=["ident_f"], writes=["ident_b"])
        kb.memset(mhalf[:], -0.5, writes=["mhalf"])
        for nm, a, b_ in (("gpre", 0, 8), ("gq", 8, 11), ("gkv", 11, 13), ("gmix", 13, 21), ("gpremlp", 21, 29)):
            kb.dma(gvec[:, a:b_], D[nm], writes=["gvec_" + nm])

        def load_w(st_tile, dst, src, ncols, gcol, gkey, wkey, col_map=None, eng="dve"):
            kb.dma(st_tile[:, 0:ncols], src, writes=[st_tile.name])
            cm = col_map or [(0, ncols, 0)]
            for (s0, s1, d0) in cm:
                if gcol is None:
                    kb.cp(dst[:, d0:d0 + (s1 - s0)], st_tile[:, s0:s1], reads=[st_tile.name], writes=[wkey])
                else:
                    kb.ts(dst[:, d0:d0 + (s1 - s0)], st_tile[:, s0:s1], gcol, None, ALU.mult,
                          reads=[st_tile.name, gkey], writes=[wkey])

        def make_p0_jobs(stg, stb):
            jobs = []
            cnt = [0]

            def up_job(k, hh):
                def f():
                    i = cnt[0] % 2
                    cnt[0] += 1
                    kb.dma(stg[i][:], D["w_up"][k * 128:(k + 1) * 128, hh * 2048:(hh + 1) * 2048], writes=[f"stg{i}"])
                    kb.ts(stb[i][:], stg[i][:], gvec[:, 21 + k:22 + k], None, ALU.mult,
                          reads=[f"stg{i}", "gvec_gpremlp"], writes=[f"stb{i}"], eng="pool")
                    kb.dma(wup_s[k * 128:(k + 1) * 128, hh * 2048:(hh + 1) * 2048], stb[i][:],
                           reads=[f"stb{i}"], writes=["wup_s"])
                return f

            def dn_job(k):
                def f():
                    i = cnt[0] % 2
                    cnt[0] += 1
                    kb.dma(stg[i][:, 0:1024], D["w_down"][k * 128:(k + 1) * 128, :], writes=[f"stg{i}"])
                    kb.cp(stb[i][:, 0:1024], stg[i][:, 0:1024], reads=[f"stg{i}"], writes=[f"stb{i}"], eng="pool")
                    kb.dma(wdn_s[k * 128:(k + 1) * 128, :], stb[i][:, 0:1024], reads=[f"stb{i}"], writes=["wdn_s"])
                return f
            for k in range(8):
                for hh in range(2):
                    jobs.append(up_job(k, hh))
            for k in range(32):
                jobs.append(dn_job(k))
            return jobs

        P.barrier()
        if stage == "p0":
            P.emit()
            return nc

        with contextlib.ExitStack() as sa:
            w_in_bf = sb(sa, "w_in_bf", [128, 8, 1184], BF16)
            w_uq_bf = sb(sa, "w_uq_bf", [128, 3, 768], BF16)
            stg = sb(sa, "stgA", [128, 1184], F32)
            for k in range(8):
                load_w(stg, w_in_bf[:, k, :], D["w_in"][k * 128:(k + 1) * 128, :], 1184, gvec[:, k:k + 1],
                       "gvec_gpre", "w_in_bf",
                       col_map=[(0, 384, 0), (640, 672, 384), (384, 640, 416), (672, 1184, 672)])
            for k in range(3):
                load_w(stg, w_uq_bf[:, k, :], D["w_uq"][k * 128:(k + 1) * 128, :], 768, gvec[:, 8 + k:9 + k],
                       "gvec_gq", "w_uq_bf")
            posi = sb(sa, "posi", [128, NT], I32)
            posf = sb(sa, "posf", [128, NT], F32)
            inv16 = sb(sa, "inv16", [128, 16], F32)
            ang = sb(sa, "ang", [128, NT, 16], F32)
            frc = sb(sa, "frc", [128, NT, 16], F32)
            cosT = sb(sa, "cosT", [128, NT, 16], F32)
            sinT = sb(sa, "sinT", [128, NT, 16], F32)
            cosq = ang
            sinq = frc
            kb.dma(posi[:], D["posT"], writes=["posi"])
            kb.dma(inv16[:], D["inv16"], writes=["inv16"])
            kb.cp(posf[:], posi[:], reads=["posi"], writes=["posf"])
            kb.ts(posf[:], posf[:], float(NMETA), None, ALU.add, reads=["posf"], writes=["posf"])
            kb.tt(ang[:], posf[:].unsqueeze(2).broadcast_to([128, NT, 16]),
                  inv16[:].unsqueeze(1).broadcast_to([128, NT, 16]), ALU.mult,
                  reads=["posf", "inv16"], writes=["ang"])
            kb.ts(ang[:], ang[:], 1.0 / TWO_PI, None, ALU.mult, reads=["ang"], writes=["ang"])
            kb.ts(frc[:], ang[:], MAGIC, MAGIC, ALU.add, ALU.subtract, reads=["ang"], writes=["frc"])
            kb.tt(frc[:], ang[:], frc[:], ALU.subtract, reads=["ang", "frc"], writes=["frc"])
            kb.act(sinT[:], frc[:], AF.Sin, scale=TWO_PI, reads=["frc"], writes=["sinT"])
            kb.act(cosT[:], frc[:], AF.Sin, scale=math.pi, reads=["frc"], writes=["cosT"])
            kb.tt(cosT[:], cosT[:], cosT[:], ALU.mult, reads=["cosT"], writes=["cosT"])
            kb.ts(cosT[:], cosT[:], -2.0, 1.0, ALU.mult, ALU.add, reads=["cosT"], writes=["cosT"])
            kb.ts(cosq[:], cosT[:], SCALE, None, ALU.mult, reads=["cosT", "ang", "frc"], writes=["ang"])
            kb.ts(sinq[:], sinT[:], SCALE, None, ALU.mult, reads=["sinT", "frc"], writes=["frc"])

            if stage == "a0":
                P.emit()
                return nc
            NB = 2
            xt = [sb(sa, f"xt{i}", [128, 1024], F32) for i in range(3)]
            junk = sb(sa, "junkA", [128, 1024], BF16)
            junkQ = sb(sa, "junkQ", [128, 384], BF16)
            xnb = [sb(sa, f"xnb{i}", [128, 1024], BF16) for i in range(NB)]
            xnT = [sb(sa, f"xnT{i}", [128, 8, 128], BF16) for i in range(NB)]
            st_ss = sb(sa, "st_ss", [128, NT], F32); st_t = sb(sa, "st_t", [128, NT], F32); st_r = sb(sa, "st_r", [128, NT], F32)
            sq_ss = sb(sa, "sq_ss", [128, NT], F32); sq_t = sb(sa, "sq_t", [128, NT], F32); sq_r = sb(sa, "sq_r", [128, NT], F32)
            sk_ss = sb(sa, "sk_ss", [128, NT], F32); sk_t = sb(sa, "sk_t", [128, NT], F32); sk_r = sb(sa, "sk_r", [128, NT], F32)
            cqn = [sb(sa, f"cqn{i}", [128, 384], BF16) for i in range(NB)]
            cqnT = [sb(sa, f"cqnT{i}", [128, 3, 128], BF16) for i in range(NB)]
            ckvn = [sb(sa, f"ckvn{i}", [128, 256], BF16) for i in range(NB)]
            krt = [sb(sa, f"krt{i}", [128, 128], BF16) for i in range(NB)]
            rt = [sb(sa, f"ropet{i}", [128, 8, 16], F32) for i in range(4)]
            qb = [sb(sa, f"qb{i}", [128, 8, 96], BF16) for i in range(NB)]
            QTt = [sb(sa, f"QTt{i}", [96, 8, 128], BF16) for i in range(NB)]
            utm = [sb(sa, f"utm{i}", [128, 512], BF16) for i in range(NB)]
            udt = [sb(sa, f"udt{i}", [128, 4, 8, 16], BF16) for i in range(NB)]
            pT = ps(sa, "pT", [128, 1024], BF16)
            pj0 = ps(sa, "pj0", [128, 512], F32)
            pj1 = ps(sa, "pj1", [128, 512], F32)
            pj2 = ps(sa, "pj2", [128, 512], F32)
            pm1 = ps(sa, "pm1", [128, 1024], BF16)
            pm2 = ps(sa, "pm2", [128, 1024], BF16)
            pq = ps(sa, "pq", [128, 1024], BF16)
            pqa = ps(sa, "pqa", [128, 512], F32)
            for i in range(NB):
                kb.memset(krt[i][:], 0.0, writes=[f"krt{i}"])

            tiles = list(range(NT)) if stage != "a_small" else [0, 1, 32, 33, 64]
            udb = [sb(sa, f"udb{i}", [128, 4, 8, 128], BF16) for i in range(2)]
            cq_sb = [sb(sa, f"cq_sb{i}", [128, 416], F32) for i in range(NB)]
            ckv_sb = [sb(sa, f"ckv_sb{i}", [128, 256], F32) for i in range(NB)]
            q_sb = [sb(sa, f"q_sb{i}", [128, 2, 384], F32) for i in range(NB)]

            def xload(it, t):
                kb.dma(xt[it % 3][:], D["xm"][t * 128:(t + 1) * 128, :], writes=[f"xt{it % 3}"])

            def stage1(it, t):
                own = 32 <= t < 64
                x3 = it % 3
                b2 = it % NB
                kb.act(junk[:], xt[x3][:], AF.Square, reads=[f"xt{x3}"], writes=["junkA", f"st_ss{t}"],
                       accum=st_ss[:, t:t + 1])
                kb.rstd(st_r[:, t:t + 1], st_ss[:, t:t + 1], st_t[:, t:t + 1], mhalf[:, 0:1], 1.0 / 1024,
                        reads=[f"st_ss{t}"], writes=[f"st_r{t}"], tmpkey=f"st_t{t}")
                kb.cp(xnb[b2][:], xt[x3][:], reads=[f"xt{x3}"], writes=[f"xnb{b2}"], eng="pool")
                for k in range(8):
                    kb.tr(pT[:, k * 128:(k + 1) * 128], xnb[b2][:, k * 128:(k + 1) * 128], ident_b[:],
                          reads=[f"xnb{b2}", "ident_b"], writes=["pT"])
                kb.cp(xnT[b2][:].rearrange("p k t -> p (k t)"), pT[:], reads=["pT"], writes=[f"xnT{b2}"], eng="act")
                c0 = 0 if own else 384
                for k in range(8):
                    kb.mm(pj0[:, c0:416], xnT[b2][:, k, :], w_in_bf[:, k, c0:416], start=(k == 0), stop=(k == 7),
                          reads=[f"xnT{b2}", "w_in_bf"], writes=["pj0"])
                for k in range(8):
                    kb.mm(pj1[:, 0:256], xnT[b2][:, k, :], w_in_bf[:, k, 416:672], start=(k == 0), stop=(k == 7),
                          reads=[f"xnT{b2}", "w_in_bf"], writes=["pj1"])
                for k in range(8):
                    kb.mm(pj2[:, 0:512], xnT[b2][:, k, :], w_in_bf[:, k, 672:1184], start=(k == 0), stop=(k == 7),
                          reads=[f"xnT{b2}", "w_in_bf"], writes=["pj2"])
                rs = st_r[:, t:t + 1]
                kb.act(cq_sb[b2][:, c0:416], pj0[:, c0:416], AF.Copy, scale=rs, reads=["pj0", f"st_r{t}"],
                       writes=[f"cq_sb{b2}"])
                kb.ts(ckv_sb[b2][:], pj1[:, 0:256], rs, None, ALU.mult, reads=["pj1", f"st_r{t}"],
                      writes=[f"ckv_sb{b2}"])
                kb.act(utm[b2][:], pj2[:, 0:512], AF.Copy, scale=rs, reads=["pj2", f"st_r{t}"], writes=[f"utm{b2}"])

            def stage2(it, t):
                own = 32 <= t < 64
                b2 = it % NB
                kb.act(junk[:, 0:256], ckv_sb[b2][:], AF.Square, reads=[f"ckv_sb{b2}"], writes=["junkA", f"sk_ss{t}"],
                       accum=sk_ss[:, t:t + 1])
                kb.rstd(sk_r[:, t:t + 1], sk_ss[:, t:t + 1], sk_t[:, t:t + 1], mhalf[:, 0:1], 1.0 / 256,
                        reads=[f"sk_ss{t}"], writes=[f"sk_r{t}"], tmpkey=f"sk_t{t}")
                kb.ts(ckvn[b2][:], ckv_sb[b2][:], sk_r[:, t:t + 1], None, ALU.mult,
                      reads=[f"ckv_sb{b2}", f"sk_r{t}"], writes=[f"ckvn{b2}"])
                for k in range(2):
                    kb.tr(pm1[:, 384 + k * 128:384 + (k + 1) * 128], ckvn[b2][:, k * 128:(k + 1) * 128], ident_b[:],
                          reads=[f"ckvn{b2}", "ident_b"], writes=["pm1_kv"])
                x1 = cq_sb[b2][:, 384:400]; x2 = cq_sb[b2][:, 400:416]
                cs = cosT[:, t, :]; sn = sinT[:, t, :]
                r0, r1, r2_, r3 = (rt[i][:, 0, :] for i in range(4))
                ck = f"cq_sb{b2}"
                kb.tt(r0, x1, cs, ALU.mult, reads=[ck, "cosT"], writes=["rt0"], eng="pool")
                kb.tt(r1, x2, sn, ALU.mult, reads=[ck, "sinT"], writes=["rt1"], eng="pool")
                kb.tt(r2_, x1, sn, ALU.mult, reads=[ck, "sinT"], writes=["rt2"], eng="pool")
                kb.tt(r3, x2, cs, ALU.mult, reads=[ck, "cosT"], writes=["rt3"], eng="pool")
                kb.tt(krt[b2][:, 64:80], r0, r1, ALU.subtract, reads=["rt0", "rt1"], writes=[f"krt{b2}"], eng="pool")
                kb.tt(krt[b2][:, 80:96], r2_, r3, ALU.add, reads=["rt2", "rt3"], writes=[f"krt{b2}"], eng="pool")
                kb.tr(pm1[:, 640:768], krt[b2][:], ident_b[:], reads=[f"krt{b2}", "ident_b"], writes=["pm1_kr"])
                kb.cp(ckvnT[:, :, t * 128:(t + 1) * 128],
                      pm1[:, 384:640].rearrange("p (k t) -> p k t", k=2), reads=["pm1_kv"], writes=[f"ckvnT{t}"])
                kb.cp(kropeT[64:96, t * 128:(t + 1) * 128], pm1[64:96, 640:768], reads=["pm1_kr"],
                      writes=[f"kropeT{t}"])
                for c in range(4):
                    kb.tr(pm2[:, c * 128:(c + 1) * 128], utm[b2][:, c * 128:(c + 1) * 128], ident_b[:],
                          reads=[f"utm{b2}", "ident_b"], writes=["pm2"])
                ug = (t // 8) % 2
                uo = (t % 8) * 16
                kb.cp(udb[ug][:, :, :, uo:uo + 16],
                      pm2[:, 0:512].rearrange("p (c j a) -> p c a j", c=4, a=8),
                      reads=["pm2"], writes=[f"udb{ug}"], eng="act")
                if t % 8 == 7 or t == tiles[-1] or (it + 1 < len(tiles) and tiles[it + 1] // 8 != t // 8):
                    jb = (t // 8) * 128
                    nj = uo + (16 if t < 64 else 2)
                    for c in range(4):
                        kb.dma(Ud[c * 128:(c + 1) * 128, :, jb:jb + nj], udb[ug][:, c, :, 0:nj],
                               reads=[f"udb{ug}"], writes=["Ud"])
                if not own:
                    return
                kb.act(junk[:, 0:384], cq_sb[b2][:, 0:384], AF.Square, reads=[ck], writes=["junkA", f"sq_ss{t}"],
                       accum=sq_ss[:, t:t + 1])
                kb.rstd(sq_r[:, t:t + 1], sq_ss[:, t:t + 1], sq_t[:, t:t + 1], mhalf[:, 0:1], 1.0 / 384,
                        reads=[f"sq_ss{t}"], writes=[f"sq_r{t}"], tmpkey=f"sq_t{t}")
                kb.ts(cqn[b2][:], cq_sb[b2][:, 0:384], sq_r[:, t:t + 1], None, ALU.mult,
                      reads=[ck, f"sq_r{t}"], writes=[f"cqn{b2}"])
                for k in range(3):
                    kb.tr(pm1[:, k * 128:(k + 1) * 128], cqn[b2][:, k * 128:(k + 1) * 128], ident_b[:],
                          reads=[f"cqn{b2}", "ident_b"], writes=["pm1_q"])
                kb.cp(cqnT[b2][:].rearrange("p k t -> p (k t)"), pm1[:, 0:384], reads=["pm1_q"],
                      writes=[f"cqnT{b2}"], eng="act")
                for hb, (pbank, pkey) in enumerate(((pqa, "pqa"), (pj1, "pj1"))):
                    for k in range(3):
                        kb.mm(pbank[:, 0:384], cqnT[b2][:, k, :], w_uq_bf[:, k, hb * 384:(hb + 1) * 384],
                              start=(k == 0), stop=(k == 2), reads=[f"cqnT{b2}", "w_uq_bf"], writes=[pkey])
                    kb.cp(q_sb[b2][:, hb, :], pbank[:, 0:384], reads=[pkey], writes=[f"q_sb{b2}"],
                          eng=("act" if hb else "dve"))
                qk = f"q_sb{b2}"
                qv = q_sb[b2][:].rearrange("p b (h d) -> p (b h) d", h=4)
                qo = qb[b2][:]
                kb.act(qo[:, :, 0:64], qv[:, :, 0:64], AF.Copy, scale=SCALE, reads=[qk], writes=[f"qb{b2}"])
                q1 = qv[:, :, 64:80]; q2 = qv[:, :, 80:96]
                csq = cosq[:, t, :].unsqueeze(1).broadcast_to([128, 8, 16])
                snq = sinq[:, t, :].unsqueeze(1).broadcast_to([128, 8, 16])
                a0, a1, a2, a3 = (rt[i][:] for i in range(4))
                kb.tt(a0, q1, csq, ALU.mult, reads=[qk, "ang"], writes=["rt0"])
                kb.tt(a1, q2, snq, ALU.mult, reads=[qk, "frc"], writes=["rt1"])
                kb.tt(a2, q1, snq, ALU.mult, reads=[qk, "frc"], writes=["rt2"], eng="pool")
                kb.tt(a3, q2, csq, ALU.mult, reads=[qk, "ang"], writes=["rt3"], eng="pool")
                kb.tt(qo[:, :, 64:80], a0, a1, ALU.subtract, reads=["rt0", "rt1"], writes=[f"qb{b2}"])
                kb.tt(qo[:, :, 80:96], a2, a3, ALU.add, reads=["rt2", "rt3"], writes=[f"qb{b2}"], eng="pool")
                for h in range(8):
                    kb.tr(pq[0:96, h * 128:(h + 1) * 128], qb[b2][:, h, :], ident_b[:],
                          reads=[f"qb{b2}", "ident_b"], writes=["pq"])
                kb.cp(QTt[b2][:].rearrange("p h t -> p (h t)"), pq[0:96, :], reads=["pq"], writes=[f"QTt{b2}"], eng="act")
                tq = t - 32
                kb.dma(QTs[:, :, tq * 128:(tq + 1) * 128], QTt[b2][:], reads=[f"QTt{b2}"], writes=["QTs"])

            xload(0, tiles[0])
            if len(tiles) > 1:
                xload(1, tiles[1])
            stage1(0, tiles[0])
            for it, t in enumerate(tiles):
                if it + 2 < len(tiles):
                    xload(it + 2, tiles[it + 2])
                if it + 1 < len(tiles):
                    stage1(it + 1, tiles[it + 1])
                stage2(it, t)

            if stage in ("a", "a_small"):
                with contextlib.ExitStack() as sd:
                    d0 = sb(sd, "dbg0", [128, 1024], F32)
                    qtb = sb(sd, "dbgq", [96, 128], BF16)
                    ub = sb(sd, "dbgu", [128, 8, 16], BF16)
                    kb.memset(d0[:], 0.0, writes=["dbg0"])
                    allk = [f"ckvnT{t}" for t in tiles] + [f"kropeT{t}" for t in tiles]
                    kb.cp(d0[:, 0:128], ckvnT[:, 0, 0:128], reads=allk + ["dbg0"], writes=["dbg0a"])
                    kb.cp(d0[:, 128:256], ckvnT[:, 1, 64 * 128:65 * 128], reads=allk + ["dbg0"], writes=["dbg0b"])
                    kb.cp(d0[64:96, 256:384], kropeT[64:96, 128:256], reads=allk + ["dbg0"], writes=["dbg0c"])
                    kb.dma(qtb[:], QTs[:, 3, 128:256], reads=["QTs"], writes=["dbgq"])
                    kb.cp(d0[0:96, 384:512], qtb[:], reads=["dbgq", "dbg0"], writes=["dbg0d"])
                    kb.dma(ub[:], Ud[128:256, :, 16:32], reads=["Ud"], writes=["dbgu"])
                    kb.cp(d0[:, 512:640], ub[:].rearrange("p a j -> p (a j)"), reads=["dbgu", "dbg0"], writes=["dbg0e"])
                    kb.dma(dbg[:, 0:1024], d0[:], reads=["dbg0a", "dbg0b", "dbg0c", "dbg0d", "dbg0e"])
                P.emit()
                return nc

        P.barrier()
        cfg = CFG
        with contextlib.ExitStack() as sc:
            w_ukv_bf = sb(sc, "w_ukv_bf", [128, 2, 1024], BF16)
            stgC = sb(sc, "stgC", [128, 1024], F32)
            for k in range(2):
                load_w(stgC, w_ukv_bf[:, k, :], D["w_ukv"][k * 128:(k + 1) * 128, :], 1024, gvec[:, 11 + k:12 + k],
                       "gvec_gkv", "w_ukv_bf")
            KT = [sb(sc, f"KT{i}", [96, NTOK], BF16) for i in range(2)]
            Vt = [sb(sc, f"Vt{i}", [128, NT, 65], BF16) for i in range(2)]
            QTb = [sb(sc, f"QTb{i}", [96, HALF], BF16) for i in range(2)]
            PT = [sb(sc, f"PT{i}", [128, 1024], BF16) for i in range(4)]
            zl = sb(sc, "zl", [128, 128], BF16)
            zr = sb(sc, "zr", [128, 260], BF16)
            rc = sb(sc, "rc", [128, 4], F32)
            pss = [ps(sc, f"pss{i}", [128, 1024], F32) for i in range(3)]
            pso = ps(sc, "pso", [128, 512], F32)
            pskv = ps(sc, "pskv", [128, 512], F32)
            stg0 = [sb(sc, f"stg{i}", [128, 2048], F32) for i in range(2)]
            stb0 = [sb(sc, f"stb{i}", [128, 2048], BF16) for i in range(2)]
            p0_jobs = make_p0_jobs(stg0, stb0)
            kb.memset(zl[:], 0.0, writes=["zl"])
            kb.memset(zr[:], 0.0, writes=["zr"])
            for i in range(2):
                kb.memset(Vt[i][:, :, 64:65], 1.0, writes=[f"Vt{i}"])
                kb.cp(KT[i][64:96, :], kropeT[64:96, :], writes=[f"KT{i}"], eng=("act" if i else "dve"))
            heads = cfg.get("heads", list(range(8)))
            qblocks = cfg.get("qblocks", list(range(8)))
            ev = 0
            def build_head(hi):
                h = heads[hi]
                hb = hi % 2
                kb.dma(QTb[hb][:], QTs[:, h, :], writes=[f"QTb{hb}"])
                for blk in range(17):
                    n0 = blk * 512
                    n = 512 if blk < 16 else 16
                    for k in range(2):
                        kb.mm(pskv[0:64, 0:n], w_ukv_bf[:, k, h * 128:h * 128 + 64], ckvnT[:, k, n0:n0 + n],
                              start=(k == 0), stop=(k == 1), reads=["w_ukv_bf"], writes=["pskv"])
                    kb.cp(KT[hb][0:64, n0:n0 + n], pskv[0:64, 0:n], reads=["pskv"], writes=[f"KT{hb}"])
                for g in range(9):
                    kts = list(range(g * 8, min(g * 8 + 8, NT)))
                    for j, kt in enumerate(kts):
                        rows = 128 if kt < 64 else 16
                        for k in range(2):
                            kb.mm(pskv[0:rows, j * 64:(j + 1) * 64], ckvnT[:, k, kt * 128:kt * 128 + rows],
                                  w_ukv_bf[:, k, h * 128 + 64:h * 128 + 128], start=(k == 0), stop=(k == 1),
                                  reads=["w_ukv_bf"], writes=["pskv"])
                    kb.cp(Vt[hb][:, kts[0]:kts[-1] + 1, 0:64],
                          pskv[:, 0:len(kts) * 64].rearrange("p (t d) -> p t d", d=64),
                          reads=["pskv"], writes=[f"Vt{hb}"])

            if heads:
                build_head(0)
            for hi, h in enumerate(heads):
                hb = hi % 2
                items = [(qb_, p) for qb_ in qblocks for p in range(33)]

                def emit_S(idx):
                    qb_, p = items[idx]
                    si = idx % 3
                    q0 = qb_ * 512
                    if p < 32:
                        for half in range(2):
                            kt = 2 * p + half
                            kb.mm(pss[si][:, half * 512:(half + 1) * 512], KT[hb][0:96, kt * 128:(kt + 1) * 128],
                                  QTb[hb][0:96, q0:q0 + 512], reads=[f"KT{hb}", f"QTb{hb}"],
                                  writes=[f"pss{si}_{half}"], banks=[f"pss{si}_{half}"])
                    else:
                        kb.mm(pss[si][0:16, 0:512], KT[hb][0:96, 8192:8208], QTb[hb][0:96, q0:q0 + 512],
                              reads=[f"KT{hb}", f"QTb{hb}"], writes=[f"pss{si}_0"], banks=[f"pss{si}_0"])

                def emit_E(idx):
                    qb_, p = items[idx]
                    si = idx % 3
                    pj = idx % 4
                    if p < 32:
                        kb.act(PT[pj][:], pss[si][:], AF.Exp, reads=[f"pss{si}_0", f"pss{si}_1"], writes=[f"PT{pj}"],
                               banks=[f"pss{si}_0", f"pss{si}_1"])
                    else:
                        kb.act(PT[pj][0:16, 0:512], pss[si][0:16, 0:512], AF.Exp, reads=[f"pss{si}_0"],
                               writes=[f"PT{pj}"], banks=[f"pss{si}_0"])

                def emit_PV(idx):
                    qb_, p = items[idx]
                    pj = idx % 4
                    if p == 0:
                        kb.mm(pso[:, 0:260], zl[:], zr[:], start=True, stop=True, reads=["zl", "zr"], writes=["pso"],
                              skip=True)
                    if p < 32:
                        for half in range(2):
                            kt = 2 * p + half
                            for qs in range(4):
                                kb.mm(pso[:, qs * 65:(qs + 1) * 65],
                                      PT[pj][:, half * 512 + qs * 128:half * 512 + (qs + 1) * 128],
                                      Vt[hb][:, kt, :], start=False, stop=False, reads=[f"PT{pj}", f"Vt{hb}"],
                                      writes=["pso"], skip=True)
                    else:
                        for qs in range(4):
                            kb.mm(pso[:, qs * 65:(qs + 1) * 65], PT[pj][0:16, qs * 128:(qs + 1) * 128],
                                  Vt[hb][0:16, 64, :], start=False, stop=True, reads=[f"PT{pj}", f"Vt{hb}"],
                                  writes=["pso"], skip=True)
                        pov = pso[:, 0:260].rearrange("p (q d) -> p q d", d=65)
                        kb.recip(rc[:], pov[:, :, 64], reads=["pso"], writes=["rc"])
                        kb.tt(attn_tm[:, qb_ * 4:(qb_ + 1) * 4, h * 64:(h + 1) * 64], pov[:, :, 0:64],
                              rc[:].unsqueeze(2).broadcast_to([128, 4, 64]), ALU.mult, reads=["pso", "rc"],
                              writes=["attn_tm"])

                n_it = len(items)
                emit_S(0)
                if n_it > 1:
                    emit_S(1)
                for i in range(n_it):
                    emit_E(i)
                    if i + 2 < n_it:
                        emit_S(i + 2)
                    emit_PV(i)
                    if i % 44 == 20 and p0_jobs:
                        p0_jobs.pop(0)()
                    if i == n_it // 2 and hi + 1 < len(heads):
                        build_head(hi + 1)
            while p0_jobs:
                p0_jobs.pop(0)()
        P.barrier()
        if stage == "c":
            with contextlib.ExitStack() as sd:
                d0 = sb(sd, "dbgc", [128, 8192], F32)
                kb.memset(d0[:], 0.0, writes=["dbgc"])
                hh = cfg.get("heads", [0])[-1]
                kb.cp(d0[:, 0:256].rearrange("p (q d) -> p q d", q=4), attn_tm[:, 0:4, hh * 64:(hh + 1) * 64],
                      reads=["dbgc"], writes=["dbgc1"])
                kb.dma(dbg, d0[:], reads=["dbgc1"])
            P.emit()
            return nc
        S2.close()
        P.barrier()

        with contextlib.ExitStack() as sbb:
            def t32o(name, shape):
                return sb(sbb, name, shape, F32)
            swp = t32o("swp", [128, 128])
            Dv = t32o("Dv", [128, 32]); flg = t32o("flg", [128, 2])
            P1c, P2c = [t32o(f"Pc{i}", [128, 512]) for i in range(2)]
            Q1b, Q2b = [t32o(f"Qb{i}", [128, 512]) for i in range(2)]
            P1g, P2g = [t32o(f"Pg{i}", [128, 512]) for i in range(2)]
            Aar, Asw = [t32o(f"Aco{i}", [128, 512]) for i in range(2)]
            for tl, nm in ((swp, "swap"), (Dv, "Dvec"), (flg, "flags")):
                kb.dma(tl[:], D[nm], writes=[tl.name])
            sb0 = contextlib.ExitStack()
            def t32(name, shape):
                return sb(sb0, name, shape, F32)
            AreT = t32("AreT", [128, 64]); AimT = t32("AimT", [128, 64]); ldt = t32("ldt", [128, 64])
            PW = t32("PWt", [128, 4, 2, 8])
            for tl, nm in ((AreT, "AreT"), (AimT, "AimT"), (ldt, "ldtB"), (PW, "PW")):
                kb.dma(tl[:], D[nm], writes=[tl.name])
            dt_ = t32("dt_", [128, 64]); ar = t32("ar", [128, 64]); wr = t32("wr", [128, 64]); wi = t32("wi", [128, 64])
            kb.act(dt_[:], ldt[:], AF.Exp, reads=[ldt.name], writes=["dt_"])
            kb.ts(ar[:], AreT[:], -1e-4, None, ALU.min, reads=[AreT.name], writes=["ar"])
            kb.tt(wr[:], ar[:], dt_[:], ALU.mult, reads=["ar", "dt_"], writes=["wr"])
            kb.tt(wi[:], AimT[:], dt_[:], ALU.mult, reads=[AimT.name, "dt_"], writes=["wi"])
            ctmp = [t32(f"ctmp{i}", [128, 512]) for i in range(4)]

            def cpow(re_o, im_o, pwv, n, tag):
                sh = list(pwv.shape)
                def V(t):
                    v = t[:, 0:n]
                    if len(sh) == 4:
                        v = v.rearrange("p (a b c) -> p a b c", a=sh[1], b=sh[2])
                    elif len(sh) == 3:
                        v = v.rearrange("p (a b) -> p a b", a=sh[1])
                    return v
                wrb = wr[:].rearrange("p (a b) -> p a b", a=2)
                wib = wi[:].rearrange("p (a b) -> p a b", a=2)
                if len(sh) == 4:
                    wrb = wrb.unsqueeze(3).broadcast_to(sh); wib = wib.unsqueeze(3).broadcast_to(sh)
                t0, t1, t2, t3 = ctmp
                kb.tt(V(t0), wrb, pwv, ALU.mult, reads=["wr", PW.name], writes=["ct0"])
                kb.act(t0[:, 0:n], t0[:, 0:n], AF.Exp, reads=["ct0"], writes=["ct0"])
                kb.tt(V(t1), wib, pwv, ALU.mult, reads=["wi", PW.name], writes=["ct1"])
                kb.ts(t1[:, 0:n], t1[:, 0:n], 1.0 / TWO_PI, None, ALU.mult, reads=["ct1"], writes=["ct1"])
                kb.ts(t2[:, 0:n], t1[:, 0:n], MAGIC, MAGIC, ALU.add, ALU.subtract, reads=["ct1"], writes=["ct2"])
                kb.tt(t1[:, 0:n], t1[:, 0:n], t2[:, 0:n], ALU.subtract, reads=["ct1", "ct2"], writes=["ct1"])
                kb.act(t2[:, 0:n], t1[:, 0:n], AF.Sin, scale=TWO_PI, reads=["ct1"], writes=["ct2"])
                kb.act(t3[:, 0:n], t1[:, 0:n], AF.Sin, scale=math.pi, reads=["ct1"], writes=["ct3"])
                kb.tt(t3[:, 0:n], t3[:, 0:n], t3[:, 0:n], ALU.mult, reads=["ct3"], writes=["ct3"])
                kb.ts(t3[:, 0:n], t3[:, 0:n], -2.0, 1.0, ALU.mult, ALU.add, reads=["ct3"], writes=["ct3"])
                kb.tt(re_o, t0[:, 0:n], t3[:, 0:n], ALU.mult, reads=["ct0", "ct3"], writes=[tag + "_re"])
                kb.tt(im_o, t0[:, 0:n], t2[:, 0:n], ALU.mult, reads=["ct0", "ct2"], writes=[tag + "_im"])

            w1r = t32("w1r", [128, 64]); w1i = t32("w1i", [128, 64])
            gr = t32("gr", [128, 64]); gi_ = t32("gi_", [128, 64])
            one_pw = t32("one_pw", [128, 2, 32])
            kb.memset(one_pw[:], 1.0, writes=["one_pw"])
            def cpow1():
                t0, t1, t2, t3 = ctmp
                n = 64
                kb.act(t0[:, 0:n], wr[:], AF.Exp, reads=["wr"], writes=["ct0"])
                kb.ts(t1[:, 0:n], wi[:], 1.0 / TWO_PI, None, ALU.mult, reads=["wi"], writes=["ct1"])
                kb.ts(t2[:, 0:n], t1[:, 0:n], MAGIC, MAGIC, ALU.add, ALU.subtract, reads=["ct1"], writes=["ct2"])
                kb.tt(t1[:, 0:n], t1[:, 0:n], t2[:, 0:n], ALU.subtract, reads=["ct1", "ct2"], writes=["ct1"])
                kb.act(t2[:, 0:n], t1[:, 0:n], AF.Sin, scale=TWO_PI, reads=["ct1"], writes=["ct2"])
                kb.act(t3[:, 0:n], t1[:, 0:n], AF.Sin, scale=math.pi, reads=["ct1"], writes=["ct3"])
                kb.tt(t3[:, 0:n], t3[:, 0:n], t3[:, 0:n], ALU.mult, reads=["ct3"], writes=["ct3"])
                kb.ts(t3[:, 0:n], t3[:, 0:n], -2.0, 1.0, ALU.mult, ALU.add, reads=["ct3"], writes=["ct3"])
                kb.tt(w1r[:], t0[:, 0:n], t3[:, 0:n], ALU.mult, reads=["ct0", "ct3"], writes=["w1r"])
                kb.tt(w1i[:], t0[:, 0:n], t2[:, 0:n], ALU.mult, reads=["ct0", "ct2"], writes=["w1i"])
            cpow1()
            g0 = t32("g0", [128, 64]); g1 = t32("g1", [128, 64]); g2 = t32("g2", [128, 64])
            kb.ts(w1r[:], w1r[:], -1.0, None, ALU.add, reads=["w1r"], writes=["w1r"])
            kb.tt(g0[:], ar[:], ar[:], ALU.mult, reads=["ar"], writes=["g0"])
            kb.tt(g1[:], AimT[:], AimT[:], ALU.mult, reads=[AimT.name], writes=["g1"])
            kb.tt(g0[:], g0[:], g1[:], ALU.add, reads=["g0", "g1"], writes=["g0"])
            kb.recip(g0[:], g0[:], reads=["g0"], writes=["g0"])
            kb.tt(g1[:], w1r[:], ar[:], ALU.mult, reads=["w1r", "ar"], writes=["g1"])
            kb.tt(g2[:], w1i[:], AimT[:], ALU.mult, reads=["w1i", AimT.name], writes=["g2"])
            kb.tt(g1[:], g1[:], g2[:], ALU.add, reads=["g1", "g2"], writes=["g1"])
            kb.tt(gr[:], g1[:], g0[:], ALU.mult, reads=["g1", "g0"], writes=["gr"])
            kb.tt(g1[:], w1i[:], ar[:], ALU.mult, reads=["w1i", "ar"], writes=["g1"])
            kb.tt(g2[:], w1r[:], AimT[:], ALU.mult, reads=["w1r", AimT.name], writes=["g2"])
            kb.tt(g1[:], g1[:], g2[:], ALU.subtract, reads=["g1", "g2"], writes=["g1"])
            kb.tt(gi_[:], g1[:], g0[:], ALU.mult, reads=["g1", "g0"], writes=["gi_"])

            def coef_tiles(name):
                return [t32(f"{name}{i}", [128, 512]) for i in range(2)]
            SH = [128, 2, 32, 8]
            def V4(t):
                return t[:].rearrange("p (a b c) -> p a b c", a=2, b=32)
            def pwview(ti):
                return PW[:, ti, :, :].unsqueeze(2).broadcast_to(SH)
            grb = gr[:].rearrange("p (a b) -> p a b", a=2).unsqueeze(3).broadcast_to(SH)
            gib = gi_[:].rearrange("p (a b) -> p a b", a=2).unsqueeze(3).broadcast_to(SH)
            cre = t32("cre", [128, 512]); cim = t32("cim", [128, 512])
            ere = t32("ere", [128, 512]); eim = t32("eim", [128, 512])

            def pform(p1, p2, re_t, im_t, tag):
                kb.cp(p1[0:64, :], re_t[0:64, :], reads=[tag + "_re"], writes=[p1.name])
                kb.ts(p1[64:128, :], im_t[64:128, :], -1.0, None, ALU.mult, reads=[tag + "_im"], writes=[p1.name])
                kb.ts(p2[0:64, :], im_t[0:64, :], -1.0, None, ALU.mult, reads=[tag + "_im"], writes=[p2.name])
                kb.ts(p2[64:128, :], re_t[64:128, :], -1.0, None, ALU.mult, reads=[tag + "_re"], writes=[p2.name])

            def qform(q1, q2, re_t, im_t, tag):
                kb.cp(q1[0:64, :], re_t[0:64, :], reads=[tag + "_re"], writes=[q1.name])
                kb.cp(q1[64:128, :], im_t[64:128, :], reads=[tag + "_im"], writes=[q1.name])
                kb.ts(q2[0:64, :], im_t[0:64, :], -1.0, None, ALU.mult, reads=[tag + "_im"], writes=[q2.name])
                kb.cp(q2[64:128, :], re_t[64:128, :], reads=[tag + "_re"], writes=[q2.name])

            def times_gamma(tag):
                t0, t1, _, _ = ctmp
                kb.tt(V4(t0), V4(cre), grb, ALU.mult, reads=[tag + "_re", "gr"], writes=["ct0"])
                kb.tt(V4(t1), V4(cim), gib, ALU.mult, reads=[tag + "_im", "gi_"], writes=["ct1"])
                kb.tt(ere[:], t0[:], t1[:], ALU.subtract, reads=["ct0", "ct1"], writes=[tag + "g_re"])
                kb.tt(V4(t0), V4(cre), gib, ALU.mult, reads=[tag + "_re", "gi_"], writes=["ct0"])
                kb.tt(V4(t1), V4(cim), grb, ALU.mult, reads=[tag + "_im", "gr"], writes=["ct1"])
                kb.tt(eim[:], t0[:], t1[:], ALU.add, reads=["ct0", "ct1"], writes=[tag + "g_im"])

            cpow(cre[:], cim[:], pwview(0), 512, "cs")
            pform(P1c, P2c, cre, cim, "cs")
            cpow(cre[:], cim[:], pwview(1), 512, "bs")
            times_gamma("bs")
            qform(Q1b, Q2b, ere, eim, "bsg")
            cpow(cre[:], cim[:], pwview(2), 512, "gg")
            times_gamma("gg")
            pform(P1g, P2g, ere, eim, "ggg")
            cpow(cre[:], cim[:], pwview(3), 512, "aa")
            kb.cp(Aar[:], cre[:], reads=["aa_re"], writes=[Aar.name])
            kb.ts(Asw[0:64, :], cim[0:64, :], -1.0, None, ALU.mult, reads=["aa_im"], writes=[Asw.name])
            kb.cp(Asw[64:128, :], cim[64:128, :], reads=["aa_im"], writes=[Asw.name])

            P.barrier()
            sb0.close()
            Cs_t = sb(sbb, "Cs_t", [128, 2, 8, 128], BF16)
            Bs_t = sb(sbb, "Bs_t", [128, 2, 8, 128], BF16)
            M0_t = sb(sbb, "M0_t", [128, 8, 128], BF16)
            psR = ps(sbb, "psR", [128, 512], F32)
            psE = ps(sbb, "psE", [128, 512], F32)
            psM = ps(sbb, "psM", [128, 512], F32)
            psT = ps(sbb, "psT", [128, 1024], BF16)
            psS = [ps(sbb, f"psS{i}", [128, 512], F32) for i in range(2)]
            psY = [ps(sbb, f"psY{i}", [128, 512], F32) for i in range(2)]
            batches = cfg.get("batches", [0, 1, 2, 3])
            U8s = [sb(sbb, f"U8_{i}", [128, 8, J], BF16) for i in range(2)]
            Y8s = [sb(sbb, f"Y8_{i}", [128, 8, 512], BF16) for i in range(2)]

            def load_u8(bi):
                ct_ = batches[bi]
                for a in range(8):
                    kb.dma(U8s[bi % 2][16 * a:16 * a + 16, :, :],
                           Ud[ct_ * 128:(ct_ + 1) * 128, a, :].rearrange("(g h) j -> h g j", h=16),
                           writes=[f"U8_{bi % 2}"], defer=True)
            load_u8(0)
            for bi, ct in enumerate(batches):
                gs = ct * 8
                U8 = U8s[bi % 2]; Y8 = Y8s[bi % 2]
                u8k = f"U8_{bi % 2}"; y8k = f"Y8_{bi % 2}"
                sT = contextlib.ExitStack()
                CrT = sb(sT, f"CrT{ct}", [128, 2, 8, 16], F32); CiT = sb(sT, f"CiT{ct}", [128, 2, 8, 16], F32)
                BrT = sb(sT, f"BrT{ct}", [128, 2, 8, 16], F32); BiT = sb(sT, f"BiT{ct}", [128, 2, 8, 16], F32)
                Bm2 = sb(sT, f"Bm2{ct}", [128, 2, 8, 16], F32)
                Gbuf = sb(sT, f"Gbuf{ct}", [128, 2, 8, 240], BF16)
                Bpad = sb(sT, f"Bpad{ct}", [128, 2, 8, 240], BF16)
                BsT = sb(sT, f"BsT{ct}", [128, 2, 8, 128], BF16)
                tA = sb(sT, f"tA{ct}", [128, 1024], F32); tB = sb(sT, f"tB{ct}", [128, 1024], F32)
                for tl, nm in ((CrT, "CrT2"), (CiT, "CiT2"), (BrT, "BrT2"), (BiT, "BiT2"), (Bm2, "Bmat2")):
                    for d in range(2):
                        kb.dma(tl[:, d, :, :], D[nm][:, d * 32 + gs:d * 32 + gs + 8, :], writes=[tl.name])
                kb.memset(Gbuf[:], 0.0, writes=["Gbuf"])
                kb.memset(Bpad[:], 0.0, writes=["Bpad"])
                for d in range(2):
                    Crb = CrT[:, d, :, :]; Cib = CiT[:, d, :, :]
                    Brb = BrT[:, d, :, :]; Bib = BiT[:, d, :, :]
                    sh4 = [128, 8, 8, 16]
                    def cf(t):
                        return V4(t)[:, d, gs:gs + 8, :].unsqueeze(3).broadcast_to(sh4)
                    def bc(v):
                        return v.unsqueeze(2).broadcast_to(sh4)
                    tAv = tA[:].rearrange("p (g k o) -> p g k o", g=8, k=8)
                    tBv = tB[:].rearrange("p (g k o) -> p g k o", g=8, k=8)
                    kb.tt(tAv, bc(Crb), cf(P1c), ALU.mult, reads=[CrT.name, P1c.name], writes=["tA"])
                    kb.tt(tBv, bc(Cib), cf(P2c), ALU.mult, reads=[CiT.name, P2c.name], writes=["tB"])
                    kb.tt(Cs_t[:, d, :, :].rearrange("p g x -> p (g x)"), tA[:], tB[:], ALU.add,
                          reads=["tA", "tB"], writes=["Cs_t"])
                    kb.tt(tAv, bc(Crb), cf(P1g), ALU.mult, reads=[CrT.name, P1g.name], writes=["tA"])
                    kb.tt(tBv, bc(Cib), cf(P2g), ALU.mult, reads=[CiT.name, P2g.name], writes=["tB"])
                    b0 = 7 * 16 if d == 0 else 0
                    kb.tt(Gbuf[:, d, :, b0:b0 + 128], tA[:].rearrange("p (g x) -> p g x", g=8),
                          tB[:].rearrange("p (g x) -> p g x", g=8), ALU.add, reads=["tA", "tB"], writes=["Gbuf"])
                    kb.cp(Bpad[:, d, :, 112:128], Bm2[:, d, :, :], reads=[Bm2.name], writes=["Bpad"])
                    kb.tt(tAv, bc(Brb), cf(Q1b), ALU.mult, reads=[BrT.name, Q1b.name], writes=["tA"])
                    kb.tt(tBv, bc(Bib), cf(Q2b), ALU.mult, reads=[BiT.name, Q2b.name], writes=["tB"])
                    kb.tt(BsT[:, d, :, :].rearrange("p g x -> p (g x)"), tA[:], tB[:], ALU.add,
                          reads=["tA", "tB"], writes=["BsT"])
                    for gi in range(8):
                        kb.tr(psT[:, gi * 128:(gi + 1) * 128], BsT[:, d, gi, :], ident_b[:], reads=["BsT", "ident_b"],
                              writes=["psT"])
                    kb.cp(Bs_t[:, d, :, :].rearrange("p g x -> p (g x)"), psT[:], reads=["psT"], writes=["Bs_t"],
                          eng="act")
                for gi in range(8):
                    n_mm = 0
                    for d in range(2):
                        for a in range(8):
                            w0 = (7 - a) * 16
                            kb.mm(psM[:, 0:128], Bpad[:, d, gi, w0:w0 + 128], Gbuf[:, d, gi, w0:w0 + 128],
                                  start=(n_mm == 0), stop=(n_mm == 15), reads=["Bpad", "Gbuf"], writes=["psM"])
                            n_mm += 1
                    kb.stt(M0_t[:, gi, :], ident_f[:], Dv[:, gs + gi:gs + gi + 1], psM[:, 0:128], ALU.mult, ALU.add,
                           reads=["ident_f", Dv.name, "psM"], writes=["M0_t"])
                P.barrier(include_deferred=False)
                sT.close()
                sR = contextlib.ExitStack()
                XL = sb(sR, f"XL{ct}", [128, 8, NL], F32); XS = sb(sR, f"XS{ct}", [128, 8, NS], F32)
                XLb = sb(sR, f"XLb{ct}", [128, 8, 512], BF16); XSb = sb(sR, f"XSb{ct}", [128, 8, 512], BF16)
                EL = sb(sR, f"EL{ct}", [128, 8, 41], F32); ES = sb(sR, f"ES{ct}", [128, 8, 21], F32)
                DL = sb(sR, f"DL{ct}", [128, 8, 40], F32); DS = sb(sR, f"DS{ct}", [128, 8, 20], F32)
                DL2 = sb(sR, f"DL2{ct}", [128, 8, 40], F32); DS2 = sb(sR, f"DS2{ct}", [128, 8, 20], F32)
                tL = sb(sR, f"tL{ct}", [128, 8, 40], F32); uL = sb(sR, f"uL{ct}", [128, 8, 40], F32)
                tS = sb(sR, f"tS{ct}", [128, 8, 20], F32); uS = sb(sR, f"uS{ct}", [128, 8, 20], F32)
                tE = sb(sR, f"tE{ct}", [128, 16], F32); uE = sb(sR, f"uE{ct}", [128, 16], F32)
                kb.memset(XL[:, :, 0:14], 0.0, writes=["XL"])
                kb.memset(XS[:, :, 514:NS], 0.0, writes=["XS"])
                ev = 0
                for gi in range(8):
                    for (d, src0, n, dst) in ((0, 0, 512, XL[:, gi, 16:528]), (0, 512, 512, XL[:, gi, 528:1040]),
                                              (1, 512, 512, XS[:, gi, 0:512])):
                        pp = psS[ev % 2]
                        kb.mm(pp[:, 0:n], Bs_t[:, d, gi, :], U8[:, gi, src0:src0 + n], reads=["Bs_t", u8k],
                              writes=[pp.name])
                        kb.cp(dst, pp[:, 0:n], reads=[pp.name], writes=["XL" if d == 0 else "XS"],
                              eng=("act" if ev % 2 else "dve"))
                        ev += 1
                    for (d, dst, fc) in ((0, XL[:, gi, 14:16], 0), (1, XS[:, gi, 512:514], 1)):
                        pp = psS[ev % 2]
                        kb.mm(pp[:, 0:2], Bs_t[:, d, gi, :], U8[:, gi, 1024:1026], reads=["Bs_t", u8k], writes=[pp.name])
                        kb.ts(dst, pp[:, 0:2], flg[:, fc:fc + 1], None, ALU.mult, reads=[pp.name, flg.name],
                              writes=["XL" if d == 0 else "XS"])
                        ev += 1
                if bi + 1 < len(batches):
                    load_u8(bi + 1)
                def coef(t, d, kk, n):
                    return V4(t)[:, d, gs:gs + 8, kk].unsqueeze(2).broadcast_to([128, 8, n])
                arL = coef(Aar, 0, 0, 40); swL = coef(Asw, 0, 0, 40)
                arS = coef(Aar, 1, 0, 20); swS = coef(Asw, 1, 0, 20)
                cfk = [Aar.name, Asw.name]
                psRL = psR[:, 0:320].rearrange("p (g s) -> p g s", g=8)
                psRS = psR[:, 320:480].rearrange("p (g s) -> p g s", g=8)
                for it in range(1, SEGL):
                    isn = SEGL - 1 - it
                    XLp = XL[:, :, it - 1::SEGL]; XLc = XL[:, :, it::SEGL]
                    XSp = XS[:, :, isn + 1::SEGL]; XSc = XS[:, :, isn::SEGL]
                    kb.mm(psRL, swp[:], XLp, reads=[swp.name, "XL"], writes=["psR"])
                    kb.mm(psRS, swp[:], XSp, reads=[swp.name, "XS"], writes=["psR"])
                    kb.tt(tL[:], XLp, arL, ALU.mult, reads=["XL"] + cfk, writes=["tL"], eng="pool")
                    kb.tt(tS[:], XSp, arS, ALU.mult, reads=["XS"] + cfk, writes=["tS"], eng="pool")
                    kb.tt(uL[:], psRL, swL, ALU.mult, reads=["psR"] + cfk, writes=["uL"])
                    kb.tt(uS[:], psRS, swS, ALU.mult, reads=["psR"] + cfk, writes=["uS"])
                    kb.tt(XLc, XLc, tL[:], ALU.add, reads=["XL", "tL"], writes=["XL"], eng="pool")
                    kb.tt(XSc, XSc, tS[:], ALU.add, reads=["XS", "tS"], writes=["XS"], eng="pool")
                    kb.tt(XLc, XLc, uL[:], ALU.add, reads=["XL", "uL"], writes=["XL"])
                    kb.tt(XSc, XSc, uS[:], ALU.add, reads=["XS", "uS"], writes=["XS"])
                a26L = V4(Aar)[:, 0, gs:gs + 8, 1]; s26L = V4(Asw)[:, 0, gs:gs + 8, 1]
                a26S = V4(Aar)[:, 1, gs:gs + 8, 1]; s26S = V4(Asw)[:, 1, gs:gs + 8, 1]
                kb.memset(EL[:, :, 0:1], 0.0, writes=["EL"])
                kb.memset(ES[:, :, 20:21], 0.0, writes=["ES"])
                for n_ in range(40):
                    kb.mm(psE[:, 0:8], swp[:], EL[:, :, n_], reads=[swp.name, "EL"], writes=["psE"])
                    kb.tt(tE[:, 0:8], EL[:, :, n_], a26L, ALU.mult, reads=["EL"] + cfk, writes=["tEL"], eng="pool")
                    kb.tt(uE[:, 0:8], psE[:, 0:8], s26L, ALU.mult, reads=["psE"] + cfk, writes=["uEL"])
                    kb.tt(tE[:, 0:8], tE[:, 0:8], XL[:, :, SEGL * n_ + SEGL - 1], ALU.add, reads=["tEL", "XL"],
                          writes=["tEL"], eng="pool")
                    kb.tt(EL[:, :, n_ + 1], tE[:, 0:8], uE[:, 0:8], ALU.add, reads=["tEL", "uEL", "EL"], writes=["EL"])
                    if n_ < 20:
                        s_ = 19 - n_
                        kb.mm(psE[:, 8:16], swp[:], ES[:, :, s_ + 1], reads=[swp.name, "ES"], writes=["psE"])
                        kb.tt(tE[:, 8:16], ES[:, :, s_ + 1], a26S, ALU.mult, reads=["ES"] + cfk, writes=["tES"], eng="pool")
                        kb.tt(uE[:, 8:16], psE[:, 8:16], s26S, ALU.mult, reads=["psE"] + cfk, writes=["uES"])
                        kb.tt(tE[:, 8:16], tE[:, 8:16], XS[:, :, SEGL * s_], ALU.add, reads=["tES", "XS"],
                              writes=["tES"], eng="pool")
                        kb.tt(ES[:, :, s_], tE[:, 8:16], uE[:, 8:16], ALU.add, reads=["tES", "uES", "ES"], writes=["ES"])
                Dl = [DL, DL2]; Ds = [DS, DS2]
                for it in range(SEGL):
                    isn = SEGL - 1 - it
                    pl = EL[:, :, 0:40] if it == 0 else Dl[(it - 1) % 2][:]
                    psv = ES[:, :, 1:21] if it == 0 else Ds[(it - 1) % 2][:]
                    dl = Dl[it % 2]; ds_ = Ds[it % 2]
                    dk = ["EL", "ES", "DLa", "DLb", "DSa", "DSb"]
                    kb.mm(psRL, swp[:], pl, reads=[swp.name] + dk, writes=["psR"])
                    kb.mm(psRS, swp[:], psv, reads=[swp.name] + dk, writes=["psR"])
                    kb.tt(tL[:], pl, arL, ALU.mult, reads=dk + cfk, writes=["tL"], eng="pool")
                    kb.tt(tS[:], psv, arS, ALU.mult, reads=dk + cfk, writes=["tS"], eng="pool")
                    kb.tt(dl[:], psRL, swL, ALU.mult, reads=["psR"] + cfk, writes=["DLa" if it % 2 == 0 else "DLb"])
                    kb.tt(ds_[:], psRS, swS, ALU.mult, reads=["psR"] + cfk, writes=["DSa" if it % 2 == 0 else "DSb"])
                    kb.tt(dl[:], dl[:], tL[:], ALU.add, reads=["tL", "DLa", "DLb"], writes=["DLa" if it % 2 == 0 else "DLb"])
                    kb.tt(ds_[:], ds_[:], tS[:], ALU.add, reads=["tS", "DSa", "DSb"], writes=["DSa" if it % 2 == 0 else "DSb"])
                    kb.tt(XL[:, :, it::SEGL], XL[:, :, it::SEGL], dl[:], ALU.add, reads=["XL", "DLa", "DLb"],
                          writes=["XL"], eng="pool")
                    kb.tt(XS[:, :, isn::SEGL], XS[:, :, isn::SEGL], ds_[:], ALU.add, reads=["XS", "DSa", "DSb"],
                          writes=["XS"], eng="pool")
                kb.cp(XLb[:], XL[:, :, 527:1039], reads=["XL"], writes=["XLb"])
                kb.cp(XSb[:], XS[:, :, 1:513], reads=["XS"], writes=["XSb"], eng="pool")
                for gi in range(8):
                    g = gs + gi
                    pp = psY[gi % 2]
                    kb.mm(pp[:, 0:512], M0_t[:, gi, :], U8[:, gi, 512:1024], start=True, stop=False,
                          reads=["M0_t", u8k], writes=[pp.name])
                    kb.mm(pp[:, 0:512], Cs_t[:, 0, gi, :], XLb[:, gi, :], start=False, stop=False,
                          reads=["Cs_t", "XLb"], writes=[pp.name])
                    kb.mm(pp[:, 0:512], Cs_t[:, 1, gi, :], XSb[:, gi, :], start=False, stop=True,
                          reads=["Cs_t", "XSb"], writes=[pp.name])
                    kb.cp(Y8[:, gi, :], pp[:, 0:512], reads=[pp.name], writes=[y8k], eng=("act" if gi % 2 else "dve"))

                for a in range(8):
                    kb.dma(Yd[ct * 128:(ct + 1) * 128, a, :].rearrange("(g h) j -> h g j", h=16),
                           Y8[16 * a:16 * a + 16, :, :], reads=[y8k], writes=["Yd"], defer=True)
                P.barrier(include_deferred=False)
                sR.close()
        P.barrier()
        if stage == "b":
            with contextlib.ExitStack() as sd:
                d0 = sb(sd, "dbgb", [128, 4096], BF16)
                d1 = sb(sd, "dbgb1", [128, 8192], F32)
                kb.dma(d0[:], Yd[0:128, :, :].rearrange("c a j -> c (a j)"), writes=["dbgb"])
                kb.memset(d1[:], 0.0, writes=["dbgb1"])
                kb.cp(d1[:, 0:4096], d0[:], reads=["dbgb", "dbgb1"], writes=["dbgb2"])
                kb.dma(dbg, d1[:], reads=["dbgb2"])
            P.emit()
            return nc

        S3 = S1.enter_context(contextlib.ExitStack())
        ssm_tm = sb(S3, "ssm_tm", [128, 32, 512], BF16)
        with contextlib.ExitStack() as sb2:
            w_glu_bf = sb(sb2, "w_glu_bf", [128, 4, 1024], BF16)
            stgG = sb(sb2, "stgG", [128, 1024], F32)
            for k in range(4):
                load_w(stgG, w_glu_bf[:, k, :], D["w_glu"][k * 128:(k + 1) * 128, :], 1024, None, None, "w_glu_bf")
            gyT = sb(sb2, "gyT", [128, 4, HALF], BF16)
            yd = [sb(sb2, f"yd{i}", [128, 8, 512], BF16) for i in range(2)]
            gt = [sb(sb2, f"gt{i}", [128, 512], F32) for i in range(2)]
            gs_ = [sb(sb2, f"gs{i}", [128, 512], F32) for i in range(2)]
            sig = [sb(sb2, f"sig{i}", [128, 512], F32) for i in range(2)]
            pz = [ps(sb2, f"pz{i}", [128, 512], F32) for i in range(4)]
            e2 = 0
            for ct in range(4):
                kb.dma(yd[ct % 2][:], Yd[ct * 128:(ct + 1) * 128, :, :], writes=[f"yd{ct % 2}"])
                for a in range(8):
                    y = yd[ct % 2][:, a, :]
                    i2 = e2 % 2
                    kb.tt(gt[i2][:], y, y, ALU.mult, reads=[f"yd{ct % 2}"], writes=[f"gt{i2}"])
                    kb.ts(gt[i2][:], gt[i2][:], 0.044715, 1.0, ALU.mult, ALU.add, reads=[f"gt{i2}"], writes=[f"gt{i2}"])
                    kb.tt(gt[i2][:], gt[i2][:], y, ALU.mult, reads=[f"gt{i2}", f"yd{ct % 2}"], writes=[f"gt{i2}"])
                    kb.act(gs_[i2][:], gt[i2][:], AF.Sigmoid, scale=1.5957691216057308, reads=[f"gt{i2}"],
                           writes=[f"gs{i2}"])
                    kb.tt(gyT[:, ct, a::8], gs_[i2][:], y, ALU.mult, reads=[f"gs{i2}", f"yd{ct % 2}"], writes=["gyT"],
                          eng="pool")
                    e2 += 1
            for tq in range(cfg.get("ntq", 32)):
                pa = pz[(tq % 2) * 2]; pb_ = pz[(tq % 2) * 2 + 1]
                for c, pp in ((0, pa), (1, pb_)):
                    for k in range(4):
                        kb.mm(pp[:, 0:512], gyT[:, k, tq * 128:(tq + 1) * 128], w_glu_bf[:, k, c * 512:(c + 1) * 512],
                              start=(k == 0), stop=(k == 3), reads=["gyT", "w_glu_bf"], writes=[pp.name])
                i2 = tq % 2
                kb.act(sig[i2][:], pb_[:, 0:512], AF.Sigmoid, reads=[pb_.name], writes=[f"sig{i2}"])
                kb.tt(ssm_tm[:, tq, :], pa[:, 0:512], sig[i2][:], ALU.mult, reads=[pa.name, f"sig{i2}"],
                      writes=["ssm_tm"])
        P.barrier()

        with contextlib.ExitStack() as sd1:
            w_out_bf = sb(sd1, "w_out_bf", [128, 8, 1024], BF16)
            stgD = sb(sd1, "stgD", [128, 1024], F32)
            for k in range(8):
                load_w(stgD, w_out_bf[:, k, :], D["w_out"][k * 128:(k + 1) * 128, :], 1024, gvec[:, 13 + k:14 + k],
                       "gvec_gmix", "w_out_bf")
            gpost_t = sb(sd1, "gpost_t", [128, 1024], F32)
            kb.dma(gpost_t[:], D["gpost"], writes=["gpost_t"])
            xt2 = [sb(sd1, f"xt2_{i}", [128, 1024], F32) for i in range(2)]
            junkD = sb(sd1, "junkD", [128, 1024], BF16)
            mixn = [sb(sd1, f"mixn{i}", [128, 1024], BF16) for i in range(2)]
            mixT = [sb(sd1, f"mixT{i}", [128, 8, 128], BF16) for i in range(2)]
            t1 = [sb(sd1, f"t1_{i}", [128, 1024], F32) for i in range(2)]
            h1 = [sb(sd1, f"h1_{i}", [128, 1024], F32) for i in range(2)]
            hn = [sb(sd1, f"hn{i}", [128, 1024], BF16) for i in range(2)]
            hnT = [sb(sd1, f"hnT{i}", [128, 8, 128], BF16) for i in range(2)]
            stt_ = {nm: sb(sd1, "d1_" + nm, [128, 32], F32) for nm in
                    ("sa", "sat", "sar", "ss", "sst", "ssr", "so", "so2", "sot", "sor", "sm", "smt", "smr")}
            pTd = ps(sd1, "pTd", [128, 1024], BF16)
            pTe = ps(sd1, "pTe", [128, 1024], BF16)
            pmo = [[ps(sd1, f"pmo{i}{c}", [128, 512], F32) for c in range(2)] for i in range(2)]
            ntq_ = cfg.get("ntq", 32)

            def d1_A(tq):
                i2 = tq % 2
                c1 = slice(tq, tq + 1)
                kb.dma(xt2[i2][:], D["xm"][(32 + tq) * 128:(33 + tq) * 128, :], writes=[f"xt2_{i2}"])
                kb.act(junkD[:, 0:512], attn_tm[:, tq, :], AF.Square, writes=["junkD", f"sa{tq}"], accum=stt_["sa"][:, c1])
                kb.rstd(stt_["sar"][:, c1], stt_["sa"][:, c1], stt_["sat"][:, c1], mhalf[:, 0:1], 1.0 / 512,
                        reads=[f"sa{tq}"], writes=[f"sar{tq}"], tmpkey=f"sat{tq}")
                kb.act(junkD[:, 0:512], ssm_tm[:, tq, :], AF.Square, writes=["junkD", f"ss{tq}"], accum=stt_["ss"][:, c1])
                kb.rstd(stt_["ssr"][:, c1], stt_["ss"][:, c1], stt_["sst"][:, c1], mhalf[:, 0:1], 1.0 / 512,
                        reads=[f"ss{tq}"], writes=[f"ssr{tq}"], tmpkey=f"sst{tq}")
                kb.ts(mixn[i2][:, 0:512], attn_tm[:, tq, :], stt_["sar"][:, c1], None, ALU.mult, reads=[f"sar{tq}"],
                      writes=[f"mixn{i2}"])
                kb.ts(mixn[i2][:, 512:1024], ssm_tm[:, tq, :], stt_["ssr"][:, c1], None, ALU.mult, reads=[f"ssr{tq}"],
                      writes=[f"mixn{i2}"], eng="pool")
                for k in range(8):
                    kb.tr(pTd[:, k * 128:(k + 1) * 128], mixn[i2][:, k * 128:(k + 1) * 128], ident_b[:],
                          reads=[f"mixn{i2}", "ident_b"], writes=["pTd"])
                kb.cp(mixT[i2][:].rearrange("p k t -> p (k t)"), pTd[:], reads=["pTd"], writes=[f"mixT{i2}"], eng="act")
                for c in range(2):
                    for k in range(8):
                        kb.mm(pmo[i2][c][:, 0:512], mixT[i2][:, k, :], w_out_bf[:, k, c * 512:(c + 1) * 512],
                              start=(k == 0), stop=(k == 7), reads=[f"mixT{i2}", "w_out_bf"], writes=[f"pmo{i2}{c}"])

            def d1_B(tq):
                i2 = tq % 2
                c1 = slice(tq, tq + 1)
                kb.act(junkD[:, 0:512], pmo[i2][0][:, 0:512], AF.Square, reads=[f"pmo{i2}0"], writes=["junkD", f"so{tq}"],
                       accum=stt_["so"][:, c1])
                kb.act(junkD[:, 512:1024], pmo[i2][1][:, 0:512], AF.Square, reads=[f"pmo{i2}1"], writes=["junkD", f"so2{tq}"],
                       accum=stt_["so2"][:, c1])
                kb.tt(stt_["so"][:, c1], stt_["so"][:, c1], stt_["so2"][:, c1], ALU.add, reads=[f"so{tq}", f"so2{tq}"],
                      writes=[f"so{tq}"])
                kb.rstd(stt_["sor"][:, c1], stt_["so"][:, c1], stt_["sot"][:, c1], mhalf[:, 0:1], 1.0 / 1024,
                        reads=[f"so{tq}"], writes=[f"sor{tq}"], tmpkey=f"sot{tq}")
                for c in range(2):
                    kb.tt(t1[i2][:, c * 512:(c + 1) * 512], pmo[i2][c][:, 0:512], gpost_t[:, c * 512:(c + 1) * 512],
                          ALU.mult, reads=[f"pmo{i2}{c}", "gpost_t"], writes=[f"t1_{i2}"])
                kb.stt(h1[i2][:], t1[i2][:], stt_["sor"][:, c1], xt2[i2][:], ALU.mult, ALU.add,
                       reads=[f"t1_{i2}", f"sor{tq}", f"xt2_{i2}"], writes=[f"h1_{i2}"])
                kb.dma(H1s[tq * 128:(tq + 1) * 128, :], h1[i2][:], reads=[f"h1_{i2}"], writes=["H1s"])
                kb.act(junkD[:], h1[i2][:], AF.Square, reads=[f"h1_{i2}"], writes=["junkD", f"sm{tq}"],
                       accum=stt_["sm"][:, c1])
                kb.rstd(stt_["smr"][:, c1], stt_["sm"][:, c1], stt_["smt"][:, c1], mhalf[:, 0:1], 1.0 / 1024,
                        reads=[f"sm{tq}"], writes=[f"smr{tq}"], tmpkey=f"smt{tq}")
                kb.ts(hn[i2][:], h1[i2][:], stt_["smr"][:, c1], None, ALU.mult, reads=[f"h1_{i2}", f"smr{tq}"],
                      writes=[f"hn{i2}"])
                for k in range(8):
                    kb.tr(pTe[:, k * 128:(k + 1) * 128], hn[i2][:, k * 128:(k + 1) * 128], ident_b[:],
                          reads=[f"hn{i2}", "ident_b"], writes=["pTe"])
                kb.cp(hnT[i2][:].rearrange("p k t -> p (k t)"), pTe[:], reads=["pTe"], writes=[f"hnT{i2}"], eng="act")
                kb.dma(HnT[:, :, tq * 128:(tq + 1) * 128], hnT[i2][:], reads=[f"hnT{i2}"], writes=["HnT"])

            if ntq_:
                d1_A(0)
            for tq in range(ntq_):
                if tq + 1 < ntq_:
                    d1_A(tq + 1)
                d1_B(tq)
        P.barrier()
        S1.close()
        P.barrier()

        with contextlib.ExitStack() as sd2:
            wdn_bf = sb(sd2, "wdn_bf", [128, 32, 1024], BF16)
            for q4 in range(4):
                kb.dma(wdn_bf[:, q4 * 8:(q4 + 1) * 8, :],
                       wdn_s[q4 * 1024:(q4 + 1) * 1024, :].rearrange("(f p) n -> p f n", p=128), writes=["wdn_bf"])
            gpm_t = sb(sd2, "gpm_t", [128, 1024], F32)
            kb.dma(gpm_t[:], D["gpostmlp"], writes=["gpm_t"])
            wup = [sb(sd2, f"wup{i}", [128, 8, 512], BF16) for i in range(3)]
            hnTb = [sb(sd2, f"hnTb{i}", [128, 8, 512], BF16) for i in range(2)]
            hid = sb(sd2, "hid", [128, 32, 512], BF16)
            rl = [sb(sd2, f"rl{i}", [128, 512], BF16) for i in range(2)]
            h1t = [sb(sd2, f"h1t{i}", [128, 1024], F32) for i in range(2)]
            t2 = [sb(sd2, f"t2_{i}", [128, 1024], F32) for i in range(2)]
            ot = [sb(sd2, f"ot{i}", [128, 1024], F32) for i in range(2)]
            junkE = sb(sd2, "junkE", [128, 1024], BF16)
            s2 = {nm: sb(sd2, "d2_" + nm, [128, 32], F32) for nm in ("a", "b", "t", "r")}
            pu = [ps(sd2, f"pu{i}", [128, 512], F32) for i in range(2)]
            pm_ = [[ps(sd2, f"pm{i}{c}", [128, 512], F32) for c in range(2)] for i in range(2)]
            nblk = cfg.get("ntq", 32) // 4
            wjobs = [(blk, fg) for blk in range(nblk) for fg in range(8)]

            def wload(j):
                blk_, fg_ = wjobs[j]
                kb.dma(wup[j % 3][:], wup_s[:, fg_ * 512:(fg_ + 1) * 512].rearrange("(k p) n -> p k n", p=128),
                       writes=[f"wup{j % 3}"])

            def hload(blk_):
                kb.dma(hnTb[blk_ % 2][:], HnT[:, :, blk_ * 512:(blk_ + 1) * 512], writes=[f"hnTb{blk_ % 2}"])

            def h1load(tq_):
                kb.dma(h1t[tq_ % 2][:], H1s[tq_ * 128:(tq_ + 1) * 128, :], writes=[f"h1t{tq_ % 2}"])

            if nblk:
                hload(0)
                wload(0)
                wload(1)
                h1load(0)
            for blk in range(nblk):
                bb = blk % 2
                if blk + 1 < nblk:
                    hload(blk + 1)
                for fg in range(8):
                    j = blk * 8 + fg
                    r_ = j % 3
                    if j + 2 < len(wjobs):
                        wload(j + 2)
                    for fi in range(4):
                        f = fg * 4 + fi
                        pp = pu[f % 2]
                        for k in range(8):
                            kb.mm(pp[:, 0:512], wup[r_][:, k, fi * 128:(fi + 1) * 128], hnTb[bb][:, k, :],
                                  start=(k == 0), stop=(k == 7), reads=[f"wup{r_}", f"hnTb{bb}"], writes=[pp.name])
                        kb.act(rl[f % 2][:], pp[:, 0:512], AF.Relu, reads=[pp.name], writes=[f"rl{f % 2}"])
                        kb.tt(hid[:, f, :], rl[f % 2][:], rl[f % 2][:], ALU.mult, reads=[f"rl{f % 2}"], writes=["hid"])
                for t4 in range(4):
                    tq = blk * 4 + t4
                    i2 = tq % 2
                    c1 = slice(tq, tq + 1)
                    if tq + 1 < nblk * 4:
                        h1load(tq + 1)
                    for c in range(2):
                        pp = pm_[i2][c]
                        for f in range(32):
                            kb.mm(pp[:, 0:512], hid[:, f, t4 * 128:(t4 + 1) * 128], wdn_bf[:, f, c * 512:(c + 1) * 512],
                                  start=(f == 0), stop=(f == 31), reads=["hid", "wdn_bf"], writes=[pp.name])
                    kb.act(junkE[:, 0:512], pm_[i2][0][:, 0:512], AF.Square, reads=[pm_[i2][0].name],
                           writes=["junkE", f"e_a{tq}"], accum=s2["a"][:, c1])
                    kb.act(junkE[:, 512:1024], pm_[i2][1][:, 0:512], AF.Square, reads=[pm_[i2][1].name],
                           writes=["junkE", f"e_b{tq}"], accum=s2["b"][:, c1])
                    kb.tt(s2["a"][:, c1], s2["a"][:, c1], s2["b"][:, c1], ALU.add, reads=[f"e_a{tq}", f"e_b{tq}"],
                          writes=[f"e_a{tq}"])
                    kb.rstd(s2["r"][:, c1], s2["a"][:, c1], s2["t"][:, c1], mhalf[:, 0:1], 1.0 / 1024,
                            reads=[f"e_a{tq}"], writes=[f"e_r{tq}"], tmpkey=f"e_t{tq}")
                    for c in range(2):
                        kb.tt(t2[i2][:, c * 512:(c + 1) * 512], pm_[i2][c][:, 0:512], gpm_t[:, c * 512:(c + 1) * 512],
                              ALU.mult, reads=[pm_[i2][c].name, "gpm_t"], writes=[f"t2_{i2}"])
                    kb.stt(ot[i2][:], t2[i2][:], s2["r"][:, c1], h1t[i2][:], ALU.mult, ALU.add,
                           reads=[f"t2_{i2}", f"e_r{tq}", f"h1t{i2}"], writes=[f"ot{i2}"])
                    kb.dma(out[tq * 128:(tq + 1) * 128, :], ot[i2][:], reads=[f"ot{i2}"], writes=["out"])

        P.emit()
    return nc


def _core_inputs(c, inp):
    b, hf = c // 2, c % 2
    x = np.asarray(inp["x"], np.float32)[b]
    meta = np.asarray(inp["meta_tokens"], np.float32)
    pos = np.asarray(inp["positions"])[b].astype(np.int32)
    metapos = np.arange(NMETA, dtype=np.int32) - NMETA
    if hf == 0:
        x = x[::-1]; pos = pos[::-1]; meta = meta[::-1]; metapos = metapos[::-1]
    xm = np.zeros((NTOK, 1024), np.float32)
    xm[:SEQ] = x
    xm[SEQ:SEQ + NMETA] = meta
    pall = np.zeros((NTOK,), np.int32)
    pall[:SEQ] = pos
    pall[SEQ:SEQ + NMETA] = metapos
    posT = np.ascontiguousarray(pall.reshape(NT, 128).T)
    inv = (1.0 / (10000.0 ** (np.arange(0, 32, 2, dtype=np.float32) / 32))).astype(np.float32)
    d = {"xm": xm, "posT": posT, "inv16": np.ascontiguousarray(np.broadcast_to(inv, (128, 16)))}
    for k in ("w_in", "w_uq", "w_ukv", "w_glu", "w_out"):
        d[k] = np.ascontiguousarray(np.asarray(inp[k], np.float32)[0])
    d["w_up"] = np.ascontiguousarray(np.asarray(inp["w_mlp_up"], np.float32)[0])
    d["w_down"] = np.ascontiguousarray(np.asarray(inp["w_mlp_down"], np.float32)[0])

    def colmaj(v, n):
        return np.ascontiguousarray(np.asarray(v, np.float32).reshape(n, 128).T)
    d["gpre"] = colmaj(inp["g_pre_mix"][0], 8)
    d["gq"] = colmaj(inp["g_q_lat"][0], 3)
    d["gkv"] = colmaj(inp["g_kv_lat"][0], 2)
    d["gmix"] = colmaj(inp["g_mix_out"][0], 8)
    d["gpremlp"] = colmaj(inp["g_pre_mlp"][0], 8)
    d["gpost"] = np.ascontiguousarray(np.broadcast_to(np.asarray(inp["g_post_mix"], np.float32)[0], (128, 1024)))
    d["gpostmlp"] = np.ascontiguousarray(np.broadcast_to(np.asarray(inp["g_post_mlp"], np.float32)[0], (128, 1024)))
    dsel = [0, 1] if hf == 1 else [1, 0]

    def dup(a):
        a = np.asarray(a, np.float32)[0][dsel]
        t_ = a.reshape(64, 64).T
        return np.ascontiguousarray(np.concatenate([t_, t_], 0))
    d["AreT"] = dup(inp["ssm_A_re"]); d["AimT"] = dup(inp["ssm_A_im"])
    ldt = np.asarray(inp["ssm_log_dt"], np.float32)[0][dsel].reshape(64)
    d["ldtB"] = np.ascontiguousarray(np.broadcast_to(ldt, (128, 64)))
    Br = np.asarray(inp["ssm_B_re"], np.float32)[0][dsel]
    Bi = np.asarray(inp["ssm_B_im"], np.float32)[0][dsel]
    Cr = np.asarray(inp["ssm_C_re"], np.float32)[0][dsel]
    Ci = np.asarray(inp["ssm_C_im"], np.float32)[0][dsel]

    def nmaj(a):
        return a.reshape(64, 64, 16).transpose(1, 0, 2)
    BrN, BiN = nmaj(Br), nmaj(Bi)
    CrN, CiN = nmaj(Cr.transpose(0, 1, 3, 2)), nmaj(Ci.transpose(0, 1, 3, 2))
    d["BrT2"] = np.ascontiguousarray(np.concatenate([BrN, BrN], 0))
    d["BiT2"] = np.ascontiguousarray(np.concatenate([BiN, BiN], 0))
    d["Bmat2"] = np.ascontiguousarray(np.concatenate([BrN, BiN], 0))
    d["CrT2"] = np.ascontiguousarray(np.concatenate([CrN, CrN], 0))
    d["CiT2"] = np.ascontiguousarray(np.concatenate([CiN, CiN], 0))
    Dv = np.asarray(inp["ssm_D"], np.float32)[0].reshape(32, 16)
    d["Dvec"] = np.ascontiguousarray(np.tile(Dv.T, (8, 1)))
    fl = np.array([1.0, 0.0] if hf == 1 else [0.0, 1.0], np.float32)
    d["flags"] = np.ascontiguousarray(np.broadcast_to(fl, (128, 2)))
    k8 = np.arange(8, dtype=np.float32)
    pw = np.zeros((4, 2, 8), np.float32)
    pw[0, 0] = k8 + 1; pw[0, 1] = 8 - k8
    pw[1, 0] = 7 - k8; pw[1, 1] = k8
    pw[2, 0] = k8; pw[2, 1] = 7 - k8
    pw[3, :, 0] = 8.0; pw[3, :, 1] = 8.0 * SEGL
    d["PW"] = np.ascontiguousarray(np.broadcast_to(pw, (128, 4, 2, 8)))
    d["ident"] = np.eye(128, dtype=np.float32)
    sw = np.zeros((128, 128), np.float32)
    for k in range(128):
        sw[k, (k + 64) % 128] = 1.0
    d["swap"] = sw
    return d


_NC_CACHE = {}


def kernel(**inputs):
    if "full" not in _NC_CACHE:
        _NC_CACHE["full"] = build_program("full")
    nc = _NC_CACHE["full"]
    in_maps = [_core_inputs(c, inputs) for c in range(8)]
    res = run_bass_kernel_spmd(nc, in_maps, core_ids=list(range(8)))
    outp = np.zeros((4, SEQ, 1024), np.float32)
    for c in range(8):
        b, hf = c // 2, c % 2
        o = np.asarray(res.results[c]["out"], np.float32)
        if hf == 1:
            outp[b, HALF:] = o
        else:
            outp[b, :HALF] = o[::-1]
    return outp
```

```python
import contextlib
import math
import numpy as np
import concourse.bass as bass
import concourse.mybir as mybir
from concourse.bass_utils import run_bass_kernel_spmd

F32 = mybir.dt.float32
BF16 = mybir.dt.bfloat16
I32 = mybir.dt.int32
ALU = mybir.AluOpType
AF = mybir.ActivationFunctionType

ENGS = ("pe", "act", "dve", "pool", "sp")
EPOCH = 12000
NDMASEM = 32
import os
NOSELF = bool(os.environ.get('NOSELF'))
DMASEM_RANGE = {"sp": (0, 16), "pool": (16, 8), "act": (24, 8)}

D_MODEL = 1024
SEQ = 8192
NMETA = 16
HALF = 4096
NT = 65
NTOK = NT * 128
NKEY = SEQ + NMETA
EPS = 1e-6
SCALE = 96 ** -0.5
J = 1026
SEGL = 26
NL = 1040
NS = 520
TWO_PI = 2.0 * math.pi
MAGIC = 12582912.0


class Prog:
    def __init__(self, nc):
        self.nc = nc
        self.ops = []
        self.last_w = {}
        self.readers = {}
        self.eng_ops = {e: [] for e in ENGS}
        self.ndma = 0
        self.ndma_eng = {}
        self.pending_barrier = {}
        self.bank_last = {}
        self.dma_since_barrier = []
        self.deferred = []
        self.dma_last_on_sem = {}

    def op(self, eng, fn, reads=(), writes=(), dma=False, aps=(), banks=None, defer=False):
        oid = len(self.ops)
        deps = set()
        if banks is None:
            banks = set()
            for a in aps:
                nm = getattr(a, "name", "")
                if isinstance(nm, str) and nm.startswith("ps_"):
                    banks.add(nm)
        for b in banks:
            la = self.bank_last.setdefault(b, {})
            for e2, o2 in la.items():
                if e2 != eng:
                    deps.add(o2)
            la[eng] = oid
        for k in reads:
            w = self.last_w.get(k)
            if w is not None:
                deps.add(w)
        for k in writes:
            w = self.last_w.get(k)
            if w is not None:
                deps.add(w)
            for r in self.readers.get(k, {}).values():
                deps.add(r)
        if eng in self.pending_barrier:
            deps |= self.pending_barrier.pop(eng)
        rec = dict(id=oid, eng=eng, fn=fn, deps=deps, dma=dma, needed=False)
        if dma:
            (self.deferred if defer else self.dma_since_barrier).append(oid)
            base, n = DMASEM_RANGE[eng]
            cnt = self.ndma_eng.get(eng, 0)
            k = base + cnt % n
            rec["dsem"] = k
            rec["dval"] = 16 * (cnt // n + 1)
            prev = self.dma_last_on_sem.get(k)
            if prev is not None:
                deps.add(prev)
            self.dma_last_on_sem[k] = oid
            self.ndma_eng[eng] = cnt + 1
            self.ndma += 1
        deps.discard(oid)
        self.ops.append(rec)
        self.eng_ops[eng].append(oid)
        for k in writes:
            self.last_w[k] = oid
            self.readers[k] = {}
        for k in reads:
            self.readers.setdefault(k, {})[("dma", oid) if dma else eng] = oid
        return oid

    def barrier(self, include_deferred=True):
        allp = set(self.dma_since_barrier)
        if include_deferred:
            allp |= set(self.deferred)
            self.deferred = []
        for e in ENGS:
            for oid in reversed(self.eng_ops[e]):
                if not self.ops[oid]["dma"]:
                    allp.add(oid)
                    break
        for e in ENGS:
            self.pending_barrier[e] = set(allp) | self.pending_barrier.get(e, set())
        self.dma_since_barrier = []

    def emit(self):
        nc = self.nc
        ops = self.ops
        for rec in ops:
            eng_max = {}
            dma_deps = []
            for d in rec["deps"]:
                p = ops[d]
                if p["dma"]:
                    dma_deps.append(d)
                else:
                    if p["eng"] == "pe" and rec["eng"] == "pe" and not rec["dma"]:
                        continue
                    if NOSELF and p["eng"] == rec["eng"] and not rec["dma"]:
                        continue
                    eng_max[p["eng"]] = max(eng_max.get(p["eng"], -1), d)
            rec["w_eng"] = eng_max
            rec["w_dma"] = dma_deps
        for e in ENGS:
            seen = {}
            seen_dma = set()
            for oid in self.eng_ops[e]:
                rec = ops[oid]
                ne = {}
                for pe_, d in rec["w_eng"].items():
                    if seen.get(pe_, -1) >= d:
                        continue
                    seen[pe_] = d
                    ne[pe_] = d
                rec["w_eng"] = ne
                nd = []
                for d in rec["w_dma"]:
                    if d in seen_dma:
                        continue
                    seen_dma.add(d)
                    nd.append(d)
                rec["w_dma"] = nd
        for rec in ops:
            for d in rec["w_eng"].values():
                ops[d]["needed"] = True
        nsem = {}
        for e in ENGS:
            c = 0
            for oid in self.eng_ops[e]:
                rec = ops[oid]
                if rec["needed"] and not rec["dma"]:
                    rec["spos"] = c
                    c += 1
            nsem[e] = (c + EPOCH - 1) // EPOCH
        with contextlib.ExitStack() as st:
            sems = {e: [st.enter_context(nc.semaphore(f"s_{e}_{i}")) for i in range(max(nsem[e], 1))]
                    for e in ENGS}
            dsems = [st.enter_context(nc.semaphore(f"s_dma_{i}")) for i in range(NDMASEM)]
            block = st.enter_context(nc.Block())
            hw = {"pe": block.tensor, "act": block.scalar, "dve": block.vector,
                  "pool": block.gpsimd, "sp": block.sync}

            def run_engine(e):
                def body(engine):
                    for oid in self.eng_ops[e]:
                        rec = ops[oid]
                        for pe_, d in rec["w_eng"].items():
                            sp = ops[d]["spos"]
                            engine.wait_ge(sems[pe_][sp // EPOCH], sp % EPOCH + 1)
                        for d in rec["w_dma"]:
                            engine.wait_ge(dsems[ops[d]["dsem"]], ops[d]["dval"])
                        ins = rec["fn"](engine)
                        if rec["dma"]:
                            ins.then_inc(dsems[rec["dsem"]], 16)
                        elif rec["needed"]:
                            sp = rec["spos"]
                            ins.then_inc(sems[e][sp // EPOCH], 1)
                    if e == "sp":
                        for k, oid in self.dma_last_on_sem.items():
                            engine.wait_ge(dsems[k], ops[oid]["dval"])
                hw[e](body)

            for e in ENGS:
                run_engine(e)


class KB:
    def __init__(self, nc):
        self.nc = nc
        self.P = Prog(nc)
        self.rr = 0

    def dma(self, out, in_, reads=(), writes=(), eng="sp", defer=False):
        self.P.op(eng, lambda e, o=out, i=in_: e.dma_start(out=o, in_=i), reads=reads, writes=writes, dma=True,
                  defer=defer)

    def mm(self, out, lhsT, rhs, start=True, stop=True, reads=(), writes=(), skip=False, banks=None):
        self.P.op("pe", lambda e, o=out, l=lhsT, r=rhs, s=start, t=stop, k=skip:
                  e.matmul(o, l, r, start=s, stop=t, skip_group_check=k), reads=reads, writes=writes,
                  aps=[out], banks=banks)

    def tr(self, out, in_, ident, reads=(), writes=(), banks=None):
        self.P.op("pe", lambda e, o=out, i=in_, d=ident: e.transpose(o, i, d), reads=reads, writes=writes,
                  aps=[out], banks=banks)

    def act(self, out, in_, func, reads=(), writes=(), scale=1.0, bias=None, accum=None, banks=None):
        def f(e, o=out, i=in_, fu=func, s=scale, b=bias, a=accum):
            kw = {}
            if b is not None:
                kw["bias"] = b
            if a is not None:
                kw["accum_out"] = a
            return e.activation(out=o, in_=i, func=fu, scale=s, **kw)
        self.P.op("act", f, reads=reads, writes=writes, aps=[out, in_], banks=banks)

    def ts(self, out, in0, s1, s2, op0, op1=None, reads=(), writes=(), eng="dve", banks=None):
        def f(e, o=out, i=in0, a=s1, b=s2, p0=op0, p1=op1):
            if p1 is None:
                return e.tensor_scalar(out=o, in0=i, scalar1=a, scalar2=None, op0=p0)
            return e.tensor_scalar(out=o, in0=i, scalar1=a, scalar2=b, op0=p0, op1=p1)
        self.P.op(eng, f, reads=reads, writes=writes, aps=[out, in0], banks=banks)

    def tt(self, out, in0, in1, op, reads=(), writes=(), eng="dve", banks=None):
        self.P.op(eng, lambda e, o=out, a=in0, b=in1, p=op: e.tensor_tensor(out=o, in0=a, in1=b, op=p),
                  reads=reads, writes=writes, aps=[out, in0, in1], banks=banks)

    def stt(self, out, in0, scalar, in1, op0, op1, reads=(), writes=(), banks=None):
        self.P.op("dve", lambda e, o=out, a=in0, s=scalar, b=in1, p0=op0, p1=op1:
                  e.scalar_tensor_tensor(out=o, in0=a, scalar=s, in1=b, op0=p0, op1=p1),
                  reads=reads, writes=writes, aps=[out, in0, in1], banks=banks)

    def cp(self, out, in_, reads=(), writes=(), eng="dve", banks=None):
        if eng == "act":
            self.act(out, in_, AF.Copy, reads=reads, writes=writes, banks=banks)
        else:
            self.P.op(eng, lambda e, o=out, i=in_: e.tensor_copy(out=o, in_=i), reads=reads, writes=writes,
                      aps=[out, in_], banks=banks)

    def memset(self, ap, val, writes=(), eng="dve"):
        self.P.op(eng, lambda e, a=ap, v=val: e.memset(a, v), writes=writes)

    def recip(self, out, in_, reads=(), writes=(), banks=None):
        self.P.op("dve", lambda e, o=out, i=in_: e.reciprocal(out=o, in_=i), reads=reads, writes=writes,
                  aps=[out, in_], banks=banks)

    def rstd(self, out, ss, tmp, mhalf, inv_n, reads, writes, tmpkey):
        self.ts(tmp, ss, inv_n, EPS, ALU.mult, ALU.add, reads=reads, writes=[tmpkey])
        self.tt(out, tmp, mhalf, ALU.pow, reads=[tmpkey], writes=writes, eng="pool")


CFG = {}


def build_program(stage="full"):
    nc = bass.Bass("TRN2", target_bir_lowering=False)
    kb = KB(nc)
    P = kb.P
    D = {}

    def inp(name, shape, dt=F32):
        D[name] = nc.dram_tensor(name, list(shape), dt, kind="ExternalInput").ap()

    inp("xm", [NTOK, 1024]); inp("posT", [128, NT], I32); inp("inv16", [128, 16])
    inp("w_in", [1024, 1184]); inp("w_uq", [384, 768]); inp("w_ukv", [256, 1024])
    inp("w_glu", [512, 1024]); inp("w_out", [1024, 1024]); inp("w_up", [1024, 4096]); inp("w_down", [4096, 1024])
    inp("gpre", [128, 8]); inp("gq", [128, 3]); inp("gkv", [128, 2]); inp("gmix", [128, 8]); inp("gpremlp", [128, 8])
    inp("gpost", [128, 1024]); inp("gpostmlp", [128, 1024])
    inp("AreT", [128, 64]); inp("AimT", [128, 64]); inp("ldtB", [128, 64])
    inp("BrT2", [128, 64, 16]); inp("BiT2", [128, 64, 16]); inp("Bmat2", [128, 64, 16])
    inp("CrT2", [128, 64, 16]); inp("CiT2", [128, 64, 16])
    inp("Dvec", [128, 32]); inp("flags", [128, 2]); inp("PW", [128, 4, 2, 8])
    inp("ident", [128, 128]); inp("swap", [128, 128])
    out = nc.dram_tensor("out", [HALF, 1024], F32, kind="ExternalOutput").ap()
    dbg = None
    if stage != "full":
        dbg = nc.dram_tensor("dbg", [128, 8192], F32, kind="ExternalOutput").ap()
    QTs = nc.dram_tensor("QTs", [96, 8, HALF], BF16, kind="Internal").ap()
    Ud = nc.dram_tensor("Ud", [512, 8, J], BF16, kind="Internal").ap()
    Yd = nc.dram_tensor("Yd", [512, 8, 512], BF16, kind="Internal").ap()
    wup_s = nc.dram_tensor("wup_s", [1024, 4096], BF16, kind="Internal").ap()
    wdn_s = nc.dram_tensor("wdn_s", [4096, 1024], BF16, kind="Internal").ap()
    HnT = nc.dram_tensor("HnT", [128, 8, HALF], BF16, kind="Internal").ap()
    H1s = nc.dram_tensor("H1s", [HALF, 1024], F32, kind="Internal").ap()

    with contextlib.ExitStack() as gst:
        def sb(st, name, shape, dt):
            return st.enter_context(nc.sbuf_tensor("sb_" + name, list(shape), dt))

        def ps(st, name, shape, dt):
            return st.enter_context(nc.psum_tensor("ps_" + name, list(shape), dt))

        ident_f = sb(gst, "ident_f", [128, 128], F32)
        ident_b = sb(gst, "ident_b", [128, 128], BF16)
        mhalf = sb(gst, "mhalf", [128, 8], F32)
        gvec = sb(gst, "gvec", [128, 32], F32)
        S1 = gst.enter_context(contextlib.ExitStack())
        attn_tm = sb(S1, "attn_tm", [128, 32, 512], BF16)
        S2 = S1.enter_context(contextlib.ExitStack())
        ckvnT = sb(S2, "ckvnT", [128, 2, NTOK], BF16)
        kropeT = sb(S2, "kropeT", [96, NTOK], BF16)
        kb.dma(ident_f[:], D["ident"], writes=["ident_f"])
        kb.cp(ident_b[:], ident_f[:], reads=["ident_f"], writes=["ident_b"])
        kb.memset(mhalf[:], -0.5, writes=["mhalf"])
        for nm, a, b_ in (("gpre", 0, 8), ("gq", 8, 11), ("gkv", 11, 13), ("gmix", 13, 21), ("gpremlp", 21, 29)):
            kb.dma(gvec[:, a:b_], D[nm], writes=["gvec_" + nm])

        def load_w(st_tile, dst, src, ncols, gcol, gkey, wkey, col_map=None, eng="dve"):
            kb.dma(st_tile[:, 0:ncols], src, writes=[st_tile.name])
            cm = col_map or [(0, ncols, 0)]
            for (s0, s1, d0) in cm:
                if gcol is None:
                    kb.cp(dst[:, d0:d0 + (s1 - s0)], st_tile[:, s0:s1], reads=[st_tile.name], writes=[wkey])
                else:
                    kb.ts(dst[:, d0:d0 + (s1 - s0)], st_tile[:, s0:s1], gcol, None, ALU.mult,
                          reads=[st_tile.name, gkey], writes=[wkey])

        def make_p0_jobs(stg, stb):
            jobs = []
            cnt = [0]

            def up_job(k, hh):
                def f():
                    i = cnt[0] % 2
                    cnt[0] += 1
                    kb.dma(stg[i][:], D["w_up"][k * 128:(k + 1) * 128, hh * 2048:(hh + 1) * 2048], writes=[f"stg{i}"])
                    kb.ts(stb[i][:], stg[i][:], gvec[:, 21 + k:22 + k], None, ALU.mult,
                          reads=[f"stg{i}", "gvec_gpremlp"], writes=[f"stb{i}"], eng="pool")
                    kb.dma(wup_s[k * 128:(k + 1) * 128, hh * 2048:(hh + 1) * 2048], stb[i][:],
                           reads=[f"stb{i}"], writes=["wup_s"])
                return f

            def dn_job(k):
                def f():
                    i = cnt[0] % 2
                    cnt[0] += 1
                    kb.dma(stg[i][:, 0:1024], D["w_down"][k * 128:(k + 1) * 128, :], writes=[f"stg{i}"])
                    kb.cp(stb[i][:, 0:1024], stg[i][:, 0:1024], reads=[f"stg{i}"], writes=[f"stb{i}"], eng="pool")
                    kb.dma(wdn_s[k * 128:(k + 1) * 128, :], stb[i][:, 0:1024], reads=[f"stb{i}"], writes=["wdn_s"])
                return f
            for k in range(8):
                for hh in range(2):
                    jobs.append(up_job(k, hh))
            for k in range(32):
                jobs.append(dn_job(k))
            return jobs

        P.barrier()
        if stage == "p0":
            P.emit()
            return nc

        with contextlib.ExitStack() as sa:
            w_in_bf = sb(sa, "w_in_bf", [128, 8, 1184], BF16)
            w_uq_bf = sb(sa, "w_uq_bf", [128, 3, 768], BF16)
            stg = sb(sa, "stgA", [128, 1184], F32)
            for k in range(8):
                load_w(stg, w_in_bf[:, k, :], D["w_in"][k * 128:(k + 1) * 128, :], 1184, gvec[:, k:k + 1],
                       "gvec_gpre", "w_in_bf",
                       col_map=[(0, 384, 0), (640, 672, 384), (384, 640, 416), (672, 1184, 672)])
            for k in range(3):
                load_w(stg, w_uq_bf[:, k, :], D["w_uq"][k * 128:(k + 1) * 128, :], 768, gvec[:, 8 + k:9 + k],
                       "gvec_gq", "w_uq_bf")
            posi = sb(sa, "posi", [128, NT], I32)
            posf = sb(sa, "posf", [128, NT], F32)
            inv16 = sb(sa, "inv16", [128, 16], F32)
            ang = sb(sa, "ang", [128, NT, 16], F32)
            frc = sb(sa, "frc", [128, NT, 16], F32)
            cosT = sb(sa, "cosT", [128, NT, 16], F32)
            sinT = sb(sa, "sinT", [128, NT, 16], F32)
            cosq = ang
            sinq = frc
            kb.dma(posi[:], D["posT"], writes=["posi"])
            kb.dma(inv16[:], D["inv16"], writes=["inv16"])
            kb.cp(posf[:], posi[:], reads=["posi"], writes=["posf"])
            kb.ts(posf[:], posf[:], float(NMETA), None, ALU.add, reads=["posf"], writes=["posf"])
            kb.tt(ang[:], posf[:].unsqueeze(2).broadcast_to([128, NT, 16]),
                  inv16[:].unsqueeze(1).broadcast_to([128, NT, 16]), ALU.mult,
                  reads=["posf", "inv16"], writes=["ang"])
            kb.ts(ang[:], ang[:], 1.0 / TWO_PI, None, ALU.mult, reads=["ang"], writes=["ang"])
            kb.ts(frc[:], ang[:], MAGIC, MAGIC, ALU.add, ALU.subtract, reads=["ang"], writes=["frc"])
            kb.tt(frc[:], ang[:], frc[:], ALU.subtract, reads=["ang", "frc"], writes=["frc"])
            kb.act(sinT[:], frc[:], AF.Sin, scale=TWO_PI, reads=["frc"], writes=["sinT"])
            kb.act(cosT[:], frc[:], AF.Sin, scale=math.pi, reads=["frc"], writes=["cosT"])
            kb.tt(cosT[:], cosT[:], cosT[:], ALU.mult, reads=["cosT"], writes=["cosT"])
            kb.ts(cosT[:], cosT[:], -2.0, 1.0, ALU.mult, ALU.add, reads=["cosT"], writes=["cosT"])
            kb.ts(cosq[:], cosT[:], SCALE, None, ALU.mult, reads=["cosT", "ang", "frc"], writes=["ang"])
            kb.ts(sinq[:], sinT[:], SCALE, None, ALU.mult, reads=["sinT", "frc"], writes=["frc"])

            if stage == "a0":
                P.emit()
                return nc
            NB = 2
            xt = [sb(sa, f"xt{i}", [128, 1024], F32) for i in range(3)]
            junk = sb(sa, "junkA", [128, 1024], BF16)
            junkQ = sb(sa, "junkQ", [128, 384], BF16)
            xnb = [sb(sa, f"xnb{i}", [128, 1024], BF16) for i in range(NB)]
            xnT = [sb(sa, f"xnT{i}", [128, 8, 128], BF16) for i in range(NB)]
            st_ss = sb(sa, "st_ss", [128, NT], F32); st_t = sb(sa, "st_t", [128, NT], F32); st_r = sb(sa, "st_r", [128, NT], F32)
            sq_ss = sb(sa, "sq_ss", [128, NT], F32); sq_t = sb(sa, "sq_t", [128, NT], F32); sq_r = sb(sa, "sq_r", [128, NT], F32)
            sk_ss = sb(sa, "sk_ss", [128, NT], F32); sk_t = sb(sa, "sk_t", [128, NT], F32); sk_r = sb(sa, "sk_r", [128, NT], F32)
            cqn = [sb(sa, f"cqn{i}", [128, 384], BF16) for i in range(NB)]
            cqnT = [sb(sa, f"cqnT{i}", [128, 3, 128], BF16) for i in range(NB)]
            ckvn = [sb(sa, f"ckvn{i}", [128, 256], BF16) for i in range(NB)]
            krt = [sb(sa, f"krt{i}", [128, 128], BF16) for i in range(NB)]
            rt = [sb(sa, f"ropet{i}", [128, 8, 16], F32) for i in range(4)]
            qb = [sb(sa, f"qb{i}", [128, 8, 96], BF16) for i in range(NB)]
            QTt = [sb(sa, f"QTt{i}", [96, 8, 128], BF16) for i in range(NB)]
            utm = [sb(sa, f"utm{i}", [128, 512], BF16) for i in range(NB)]
            udt = [sb(sa, f"udt{i}", [128, 4, 8, 16], BF16) for i in range(NB)]
            pT = ps(sa, "pT", [128, 1024], BF16)
            pj0 = ps(sa, "pj0", [128, 512], F32)
            pj1 = ps(sa, "pj1", [128, 512], F32)
            pj2 = ps(sa, "pj2", [128, 512], F32)
            pm1 = ps(sa, "pm1", [128, 1024], BF16)
            pm2 = ps(sa, "pm2", [128, 1024], BF16)
            pq = ps(sa, "pq", [128, 1024], BF16)
            pqa = ps(sa, "pqa", [128, 512], F32)
            for i in range(NB):
                kb.memset(krt[i][:], 0.0, writes=[f"krt{i}"])

            tiles = list(range(NT)) if stage != "a_small" else [0, 1, 32, 33, 64]
            udb = [sb(sa, f"udb{i}", [128, 4, 8, 128], BF16) for i in range(2)]
            cq_sb = [sb(sa, f"cq_sb{i}", [128, 416], F32) for i in range(NB)]
            ckv_sb = [sb(sa, f"ckv_sb{i}", [128, 256], F32) for i in range(NB)]
            q_sb = [sb(sa, f"q_sb{i}", [128, 2, 384], F32) for i in range(NB)]

            def xload(it, t):
                kb.dma(xt[it % 3][:], D["xm"][t * 128:(t + 1) * 128, :], writes=[f"xt{it % 3}"])

            def stage1(it, t):
                own = 32 <= t < 64
                x3 = it % 3
                b2 = it % NB
                kb.act(junk[:], xt[x3][:], AF.Square, reads=[f"xt{x3}"], writes=["junkA", f"st_ss{t}"],
                       accum=st_ss[:, t:t + 1])
                kb.rstd(st_r[:, t:t + 1], st_ss[:, t:t + 1], st_t[:, t:t + 1], mhalf[:, 0:1], 1.0 / 1024,
                        reads=[f"st_ss{t}"], writes=[f"st_r{t}"], tmpkey=f"st_t{t}")
                kb.cp(xnb[b2][:], xt[x3][:], reads=[f"xt{x3}"], writes=[f"xnb{b2}"])
                for k in range(8):
                    kb.tr(pT[:, k * 128:(k + 1) * 128], xnb[b2][:, k * 128:(k + 1) * 128], ident_b[:],
                          reads=[f"xnb{b2}", "ident_b"], writes=["pT"])
                kb.cp(xnT[b2][:].rearrange("p k t -> p (k t)"), pT[:], reads=["pT"], writes=[f"xnT{b2}"], eng="act")
                c0 = 0 if own else 384
                for k in range(8):
                    kb.mm(pj0[:, c0:416], xnT[b2][:, k, :], w_in_bf[:, k, c0:416], start=(k == 0), stop=(k == 7),
                          reads=[f"xnT{b2}", "w_in_bf"], writes=["pj0"])
                for k in range(8):
                    kb.mm(pj1[:, 0:256], xnT[b2][:, k, :], w_in_bf[:, k, 416:672], start=(k == 0), stop=(k == 7),
                          reads=[f"xnT{b2}", "w_in_bf"], writes=["pj1"])
                for k in range(8):
                    kb.mm(pj2[:, 0:512], xnT[b2][:, k, :], w_in_bf[:, k, 672:1184], start=(k == 0), stop=(k == 7),
                          reads=[f"xnT{b2}", "w_in_bf"], writes=["pj2"])
                rs = st_r[:, t:t + 1]
                kb.act(cq_sb[b2][:, c0:416], pj0[:, c0:416], AF.Copy, scale=rs, reads=["pj0", f"st_r{t}"],
                       writes=[f"cq_sb{b2}"])
                kb.ts(ckv_sb[b2][:], pj1[:, 0:256], rs, None, ALU.mult, reads=["pj1", f"st_r{t}"],
                      writes=[f"ckv_sb{b2}"])
                kb.act(utm[b2][:], pj2[:, 0:512], AF.Copy, scale=rs, reads=["pj2", f"st_r{t}"], writes=[f"utm{b2}"])

            def stage2(it, t):
                own = 32 <= t < 64
                b2 = it % NB
                kb.act(junk[:, 0:256], ckv_sb[b2][:], AF.Square, reads=[f"ckv_sb{b2}"], writes=["junkA", f"sk_ss{t}"],
                       accum=sk_ss[:, t:t + 1])
                kb.rstd(sk_r[:, t:t + 1], sk_ss[:, t:t + 1], sk_t[:, t:t + 1], mhalf[:, 0:1], 1.0 / 256,
                        reads=[f"sk_ss{t}"], writes=[f"sk_r{t}"], tmpkey=f"sk_t{t}")
                kb.ts(ckvn[b2][:], ckv_sb[b2][:], sk_r[:, t:t + 1], None, ALU.mult,
                      reads=[f"ckv_sb{b2}", f"sk_r{t}"], writes=[f"ckvn{b2}"])
                for k in range(2):
                    kb.tr(pm1[:, 384 + k * 128:384 + (k + 1) * 128], ckvn[b2][:, k * 128:(k + 1) * 128], ident_b[:],
                          reads=[f"ckvn{b2}", "ident_b"], writes=["pm1_kv"])
                x1 = cq_sb[b2][:, 384:400]; x2 = cq_sb[b2][:, 400:416]
                cs = cosT[:, t, :]; sn = sinT[:, t, :]
                r0, r1, r2_, r3 = (rt[i][:, 0, :] for i in range(4))
                ck = f"cq_sb{b2}"
                kb.tt(r0, x1, cs, ALU.mult, reads=[ck, "cosT"], writes=["rt0"], eng="pool")
                kb.tt(r1, x2, sn, ALU.mult, reads=[ck, "sinT"], writes=["rt1"], eng="pool")
                kb.tt(r2_, x1, sn, ALU.mult, reads=[ck, "sinT"], writes=["rt2"], eng="pool")
                kb.tt(r3, x2, cs, ALU.mult, reads=[ck, "cosT"], writes=["rt3"], eng="pool")
                kb.tt(krt[b2][:, 64:80], r0, r1, ALU.subtract, reads=["rt0", "rt1"], writes=[f"krt{b2}"], eng="pool")
                kb.tt(krt[b2][:, 80:96], r2_, r3, ALU.add, reads=["rt2", "rt3"], writes=[f"krt{b2}"], eng="pool")
                kb.tr(pm1[:, 640:768], krt[b2][:], ident_b[:], reads=[f"krt{b2}", "ident_b"], writes=["pm1_kr"])
                kb.cp(ckvnT[:, :, t * 128:(t + 1) * 128],
                      pm1[:, 384:640].rearrange("p (k t) -> p k t", k=2), reads=["pm1_kv"], writes=[f"ckvnT{t}"])
                kb.cp(kropeT[64:96, t * 128:(t + 1) * 128], pm1[64:96, 640:768], reads=["pm1_kr"],
                      writes=[f"kropeT{t}"])
                for c in range(4):
                    kb.tr(pm2[:, c * 128:(c + 1) * 128], utm[b2][:, c * 128:(c + 1) * 128], ident_b[:],
                          reads=[f"utm{b2}", "ident_b"], writes=["pm2"])
                ug = (t // 8) % 2
                uo = (t % 8) * 16
                kb.cp(udb[ug][:, :, :, uo:uo + 16],
                      pm2[:, 0:512].rearrange("p (c j a) -> p c a j", c=4, a=8),
                      reads=["pm2"], writes=[f"udb{ug}"], eng="act")
                if t % 8 == 7 or t == tiles[-1] or (it + 1 < len(tiles) and tiles[it + 1] // 8 != t // 8):
                    jb = (t // 8) * 128
                    nj = uo + (16 if t < 64 else 2)
                    for c in range(4):
                        kb.dma(Ud[c * 128:(c + 1) * 128, :, jb:jb + nj], udb[ug][:, c, :, 0:nj],
                               reads=[f"udb{ug}"], writes=["Ud"])
                if not own:
                    return
                kb.act(junk[:, 0:384], cq_sb[b2][:, 0:384], AF.Square, reads=[ck], writes=["junkA", f"sq_ss{t}"],
                       accum=sq_ss[:, t:t + 1])
                kb.rstd(sq_r[:, t:t + 1], sq_ss[:, t:t + 1], sq_t[:, t:t + 1], mhalf[:, 0:1], 1.0 / 384,
                        reads=[f"sq_ss{t}"], writes=[f"sq_r{t}"], tmpkey=f"sq_t{t}")
                kb.ts(cqn[b2][:], cq_sb[b2][:, 0:384], sq_r[:, t:t + 1], None, ALU.mult,
                      reads=[ck, f"sq_r{t}"], writes=[f"cqn{b2}"])
                for k in range(3):
                    kb.tr(pm1[:, k * 128:(k + 1) * 128], cqn[b2][:, k * 128:(k + 1) * 128], ident_b[:],
                          reads=[f"cqn{b2}", "ident_b"], writes=["pm1_q"])
                kb.cp(cqnT[b2][:].rearrange("p k t -> p (k t)"), pm1[:, 0:384], reads=["pm1_q"],
                      writes=[f"cqnT{b2}"], eng="act")
                for hb, (pbank, pkey) in enumerate(((pqa, "pqa"), (pj1, "pj1"))):
                    for k in range(3):
                        kb.mm(pbank[:, 0:384], cqnT[b2][:, k, :], w_uq_bf[:, k, hb * 384:(hb + 1) * 384],
                              start=(k == 0), stop=(k == 2), reads=[f"cqnT{b2}", "w_uq_bf"], writes=[pkey])
                    kb.cp(q_sb[b2][:, hb, :], pbank[:, 0:384], reads=[pkey], writes=[f"q_sb{b2}"],
                          eng=("act" if hb else "dve"))
                qk = f"q_sb{b2}"
                qv = q_sb[b2][:].rearrange("p b (h d) -> p (b h) d", h=4)
                qo = qb[b2][:]
                kb.act(qo[:, :, 0:64], qv[:, :, 0:64], AF.Copy, scale=SCALE, reads=[qk], writes=[f"qb{b2}"])
                q1 = qv[:, :, 64:80]; q2 = qv[:, :, 80:96]
                csq = cosq[:, t, :].unsqueeze(1).broadcast_to([128, 8, 16])
                snq = sinq[:, t, :].unsqueeze(1).broadcast_to([128, 8, 16])
                a0, a1, a2, a3 = (rt[i][:] for i in range(4))
                kb.tt(a0, q1, csq, ALU.mult, reads=[qk, "ang"], writes=["rt0"])
                kb.tt(a1, q2, snq, ALU.mult, reads=[qk, "frc"], writes=["rt1"])
                kb.tt(a2, q1, snq, ALU.mult, reads=[qk, "frc"], writes=["rt2"], eng="pool")
                kb.tt(a3, q2, csq, ALU.mult, reads=[qk, "ang"], writes=["rt3"], eng="pool")
                kb.tt(qo[:, :, 64:80], a0, a1, ALU.subtract, reads=["rt0", "rt1"], writes=[f"qb{b2}"])
                kb.tt(qo[:, :, 80:96], a2, a3, ALU.add, reads=["rt2", "rt3"], writes=[f"qb{b2}"], eng="pool")
                for h in range(8):
                    kb.tr(pq[0:96, h * 128:(h + 1) * 128], qb[b2][:, h, :], ident_b[:],
                          reads=[f"qb{b2}", "ident_b"], writes=["pq"])
                kb.cp(QTt[b2][:].rearrange("p h t -> p (h t)"), pq[0:96, :], reads=["pq"], writes=[f"QTt{b2}"], eng="act")
                tq = t - 32
                kb.dma(QTs[:, :, tq * 128:(tq + 1) * 128], QTt[b2][:], reads=[f"QTt{b2}"], writes=["QTs"])

            xload(0, tiles[0])
            if len(tiles) > 1:
                xload(1, tiles[1])
            stage1(0, tiles[0])
            for it, t in enumerate(tiles):
                if it + 2 < len(tiles):
                    xload(it + 2, tiles[it + 2])
                if it + 1 < len(tiles):
                    stage1(it + 1, tiles[it + 1])
                stage2(it, t)

            if stage in ("a", "a_small"):
                with contextlib.ExitStack() as sd:
                    d0 = sb(sd, "dbg0", [128, 1024], F32)
                    qtb = sb(sd, "dbgq", [96, 128], BF16)
                    ub = sb(sd, "dbgu", [128, 8, 16], BF16)
                    kb.memset(d0[:], 0.0, writes=["dbg0"])
                    allk = [f"ckvnT{t}" for t in tiles] + [f"kropeT{t}" for t in tiles]
                    kb.cp(d0[:, 0:128], ckvnT[:, 0, 0:128], reads=allk + ["dbg0"], writes=["dbg0a"])
                    kb.cp(d0[:, 128:256], ckvnT[:, 1, 64 * 128:65 * 128], reads=allk + ["dbg0"], writes=["dbg0b"])
                    kb.cp(d0[64:96, 256:384], kropeT[64:96, 128:256], reads=allk + ["dbg0"], writes=["dbg0c"])
                    kb.dma(qtb[:], QTs[:, 3, 128:256], reads=["QTs"], writes=["dbgq"])
                    kb.cp(d0[0:96, 384:512], qtb[:], reads=["dbgq", "dbg0"], writes=["dbg0d"])
                    kb.dma(ub[:], Ud[128:256, :, 16:32], reads=["Ud"], writes=["dbgu"])
                    kb.cp(d0[:, 512:640], ub[:].rearrange("p a j -> p (a j)"), reads=["dbgu", "dbg0"], writes=["dbg0e"])
                    kb.dma(dbg[:, 0:1024], d0[:], reads=["dbg0a", "dbg0b", "dbg0c", "dbg0d", "dbg0e"])
                P.emit()
                return nc

        P.barrier()
        cfg = CFG
        with contextlib.ExitStack() as sc:
            w_ukv_bf = sb(sc, "w_ukv_bf", [128, 2, 1024], BF16)
            stgC = sb(sc, "stgC", [128, 1024], F32)
            for k in range(2):
                load_w(stgC, w_ukv_bf[:, k, :], D["w_ukv"][k * 128:(k + 1) * 128, :], 1024, gvec[:, 11 + k:12 + k],
                       "gvec_gkv", "w_ukv_bf")
            KT = [sb(sc, f"KT{i}", [96, NTOK], BF16) for i in range(2)]
            Vt = [sb(sc, f"Vt{i}", [128, NT, 65], BF16) for i in range(2)]
            QTb = [sb(sc, f"QTb{i}", [96, HALF], BF16) for i in range(2)]
            PT = [sb(sc, f"PT{i}", [128, 1024], BF16) for i in range(4)]
            zl = sb(sc, "zl", [128, 128], BF16)
            zr = sb(sc, "zr", [128, 260], BF16)
            rc = sb(sc, "rc", [128, 4], F32)
            pss = [ps(sc, f"pss{i}", [128, 1024], F32) for i in range(3)]
            pso = ps(sc, "pso", [128, 512], F32)
            pskv = ps(sc, "pskv", [128, 512], F32)
            stg0 = [sb(sc, f"stg{i}", [128, 2048], F32) for i in range(2)]
            stb0 = [sb(sc, f"stb{i}", [128, 2048], BF16) for i in range(2)]
            p0_jobs = make_p0_jobs(stg0, stb0)
            kb.memset(zl[:], 0.0, writes=["zl"])
            kb.memset(zr[:], 0.0, writes=["zr"])
            for i in range(2):
                kb.memset(Vt[i][:, :, 64:65], 1.0, writes=[f"Vt{i}"])
                kb.cp(KT[i][64:96, :], kropeT[64:96, :], writes=[f"KT{i}"], eng=("act" if i else "dve"))
            heads = cfg.get("heads", list(range(8)))
            qblocks = cfg.get("qblocks", list(range(8)))
            ev = 0
            def build_head(hi):
                h = heads[hi]
                hb = hi % 2
                kb.dma(QTb[hb][:], QTs[:, h, :], writes=[f"QTb{hb}"])
                for blk in range(17):
                    n0 = blk * 512
                    n = 512 if blk < 16 else 16
                    for k in range(2):
                        kb.mm(pskv[0:64, 0:n], w_ukv_bf[:, k, h * 128:h * 128 + 64], ckvnT[:, k, n0:n0 + n],
                              start=(k == 0), stop=(k == 1), reads=["w_ukv_bf"], writes=["pskv"])
                    kb.cp(KT[hb][0:64, n0:n0 + n], pskv[0:64, 0:n], reads=["pskv"], writes=[f"KT{hb}"])
                for g in range(9):
                    kts = list(range(g * 8, min(g * 8 + 8, NT)))
                    for j, kt in enumerate(kts):
                        rows = 128 if kt < 64 else 16
                        for k in range(2):
                            kb.mm(pskv[0:rows, j * 64:(j + 1) * 64], ckvnT[:, k, kt * 128:kt * 128 + rows],
                                  w_ukv_bf[:, k, h * 128 + 64:h * 128 + 128], start=(k == 0), stop=(k == 1),
                                  reads=["w_ukv_bf"], writes=["pskv"])
                    kb.cp(Vt[hb][:, kts[0]:kts[-1] + 1, 0:64],
                          pskv[:, 0:len(kts) * 64].rearrange("p (t d) -> p t d", d=64),
                          reads=["pskv"], writes=[f"Vt{hb}"])

            if heads:
                build_head(0)
            for hi, h in enumerate(heads):
                hb = hi % 2
                items = [(qb_, p) for qb_ in qblocks for p in range(33)]

                def emit_S(idx):
                    qb_, p = items[idx]
                    si = idx % 3
                    q0 = qb_ * 512
                    if p < 32:
                        for half in range(2):
                            kt = 2 * p + half
                            kb.mm(pss[si][:, half * 512:(half + 1) * 512], KT[hb][0:96, kt * 128:(kt + 1) * 128],
                                  QTb[hb][0:96, q0:q0 + 512], reads=[f"KT{hb}", f"QTb{hb}"],
                                  writes=[f"pss{si}_{half}"], banks=[f"pss{si}_{half}"])
                    else:
                        kb.mm(pss[si][0:16, 0:512], KT[hb][0:96, 8192:8208], QTb[hb][0:96, q0:q0 + 512],
                              reads=[f"KT{hb}", f"QTb{hb}"], writes=[f"pss{si}_0"], banks=[f"pss{si}_0"])

                def emit_E(idx):
                    qb_, p = items[idx]
                    si = idx % 3
                    pj = idx % 4
                    if p < 32:
                        kb.act(PT[pj][:], pss[si][:], AF.Exp, reads=[f"pss{si}_0", f"pss{si}_1"], writes=[f"PT{pj}"],
                               banks=[f"pss{si}_0", f"pss{si}_1"])
                    else:
                        kb.act(PT[pj][0:16, 0:512], pss[si][0:16, 0:512], AF.Exp, reads=[f"pss{si}_0"],
                               writes=[f"PT{pj}"], banks=[f"pss{si}_0"])

                def emit_PV(idx):
                    qb_, p = items[idx]
                    pj = idx % 4
                    if p == 0:
                        kb.mm(pso[:, 0:260], zl[:], zr[:], start=True, stop=True, reads=["zl", "zr"], writes=["pso"],
                              skip=True)
                    if p < 32:
                        for half in range(2):
                            kt = 2 * p + half
                            for qs in range(4):
                                kb.mm(pso[:, qs * 65:(qs + 1) * 65],
                                      PT[pj][:, half * 512 + qs * 128:half * 512 + (qs + 1) * 128],
                                      Vt[hb][:, kt, :], start=False, stop=False, reads=[f"PT{pj}", f"Vt{hb}"],
                                      writes=["pso"], skip=True)
                    else:
                        for qs in range(4):
                            kb.mm(pso[:, qs * 65:(qs + 1) * 65], PT[pj][0:16, qs * 128:(qs + 1) * 128],
                                  Vt[hb][0:16, 64, :], start=False, stop=True, reads=[f"PT{pj}", f"Vt{hb}"],
                                  writes=["pso"], skip=True)
                        pov = pso[:, 0:260].rearrange("p (q d) -> p q d", d=65)
                        kb.recip(rc[:], pov[:, :, 64], reads=["pso"], writes=["rc"])
                        kb.tt(attn_tm[:, qb_ * 4:(qb_ + 1) * 4, h * 64:(h + 1) * 64], pov[:, :, 0:64],
                              rc[:].unsqueeze(2).broadcast_to([128, 4, 64]), ALU.mult, reads=["pso", "rc"],
                              writes=["attn_tm"])

                n_it = len(items)
                emit_S(0)
                if n_it > 1:
                    emit_S(1)
                for i in range(n_it):
                    emit_E(i)
                    if i + 2 < n_it:
                        emit_S(i + 2)
                    emit_PV(i)
                    if i % 44 == 20 and p0_jobs:
                        p0_jobs.pop(0)()
                    if i == n_it // 2 and hi + 1 < len(heads):
                        build_head(hi + 1)
            while p0_jobs:
                p0_jobs.pop(0)()
        P.barrier()
        if stage == "c":
            with contextlib.ExitStack() as sd:
                d0 = sb(sd, "dbgc", [128, 8192], F32)
                kb.memset(d0[:], 0.0, writes=["dbgc"])
                hh = cfg.get("heads", [0])[-1]
                kb.cp(d0[:, 0:256].rearrange("p (q d) -> p q d", q=4), attn_tm[:, 0:4, hh * 64:(hh + 1) * 64],
                      reads=["dbgc"], writes=["dbgc1"])
                kb.dma(dbg, d0[:], reads=["dbgc1"])
            P.emit()
            return nc
        S2.close()
        P.barrier()

        with contextlib.ExitStack() as sbb:
            def t32o(name, shape):
                return sb(sbb, name, shape, F32)
            swp = t32o("swp", [128, 128])
            Dv = t32o("Dv", [128, 32]); flg = t32o("flg", [128, 2])
            P1c, P2c = [t32o(f"Pc{i}", [128, 512]) for i in range(2)]
            Q1b, Q2b = [t32o(f"Qb{i}", [128, 512]) for i in range(2)]
            P1g, P2g = [t32o(f"Pg{i}", [128, 512]) for i in range(2)]
            Aar, Asw = [t32o(f"Aco{i}", [128, 512]) for i in range(2)]
            for tl, nm in ((swp, "swap"), (Dv, "Dvec"), (flg, "flags")):
                kb.dma(tl[:], D[nm], writes=[tl.name])
            sb0 = contextlib.ExitStack()
            def t32(name, shape):
                return sb(sb0, name, shape, F32)
            AreT = t32("AreT", [128, 64]); AimT = t32("AimT", [128, 64]); ldt = t32("ldt", [128, 64])
            PW = t32("PWt", [128, 4, 2, 8])
            for tl, nm in ((AreT, "AreT"), (AimT, "AimT"), (ldt, "ldtB"), (PW, "PW")):
                kb.dma(tl[:], D[nm], writes=[tl.name])
            dt_ = t32("dt_", [128, 64]); ar = t32("ar", [128, 64]); wr = t32("wr", [128, 64]); wi = t32("wi", [128, 64])
            kb.act(dt_[:], ldt[:], AF.Exp, reads=[ldt.name], writes=["dt_"])
            kb.ts(ar[:], AreT[:], -1e-4, None, ALU.min, reads=[AreT.name], writes=["ar"])
            kb.tt(wr[:], ar[:], dt_[:], ALU.mult, reads=["ar", "dt_"], writes=["wr"])
            kb.tt(wi[:], AimT[:], dt_[:], ALU.mult, reads=[AimT.name, "dt_"], writes=["wi"])
            ctmp = [t32(f"ctmp{i}", [128, 512]) for i in range(4)]

            def cpow(re_o, im_o, pwv, n, tag):
                sh = list(pwv.shape)
                def V(t):
                    v = t[:, 0:n]
                    if len(sh) == 4:
                        v = v.rearrange("p (a b c) -> p a b c", a=sh[1], b=sh[2])
                    elif len(sh) == 3:
                        v = v.rearrange("p (a b) -> p a b", a=sh[1])
                    return v
                wrb = wr[:].rearrange("p (a b) -> p a b", a=2)
                wib = wi[:].rearrange("p (a b) -> p a b", a=2)
                if len(sh) == 4:
                    wrb = wrb.unsqueeze(3).broadcast_to(sh); wib = wib.unsqueeze(3).broadcast_to(sh)
                t0, t1, t2, t3 = ctmp
                kb.tt(V(t0), wrb, pwv, ALU.mult, reads=["wr", PW.name], writes=["ct0"])
                kb.act(t0[:, 0:n], t0[:, 0:n], AF.Exp, reads=["ct0"], writes=["ct0"])
                kb.tt(V(t1), wib, pwv, ALU.mult, reads=["wi", PW.name], writes=["ct1"])
                kb.ts(t1[:, 0:n], t1[:, 0:n], 1.0 / TWO_PI, None, ALU.mult, reads=["ct1"], writes=["ct1"])
                kb.ts(t2[:, 0:n], t1[:, 0:n], MAGIC, MAGIC, ALU.add, ALU.subtract, reads=["ct1"], writes=["ct2"])
                kb.tt(t1[:, 0:n], t1[:, 0:n], t2[:, 0:n], ALU.subtract, reads=["ct1", "ct2"], writes=["ct1"])
                kb.act(t2[:, 0:n], t1[:, 0:n], AF.Sin, scale=TWO_PI, reads=["ct1"], writes=["ct2"])
                kb.act(t3[:, 0:n], t1[:, 0:n], AF.Sin, scale=math.pi, reads=["ct1"], writes=["ct3"])
                kb.tt(t3[:, 0:n], t3[:, 0:n], t3[:, 0:n], ALU.mult, reads=["ct3"], writes=["ct3"])
                kb.ts(t3[:, 0:n], t3[:, 0:n], -2.0, 1.0, ALU.mult, ALU.add, reads=["ct3"], writes=["ct3"])
                kb.tt(re_o, t0[:, 0:n], t3[:, 0:n], ALU.mult, reads=["ct0", "ct3"], writes=[tag + "_re"])
                kb.tt(im_o, t0[:, 0:n], t2[:, 0:n], ALU.mult, reads=["ct0", "ct2"], writes=[tag + "_im"])

            w1r = t32("w1r", [128, 64]); w1i = t32("w1i", [128, 64])
            gr = t32("gr", [128, 64]); gi_ = t32("gi_", [128, 64])
            one_pw = t32("one_pw", [128, 2, 32])
            kb.memset(one_pw[:], 1.0, writes=["one_pw"])
            def cpow1():
                t0, t1, t2, t3 = ctmp
                n = 64
                kb.act(t0[:, 0:n], wr[:], AF.Exp, reads=["wr"], writes=["ct0"])
                kb.ts(t1[:, 0:n], wi[:], 1.0 / TWO_PI, None, ALU.mult, reads=["wi"], writes=["ct1"])
                kb.ts(t2[:, 0:n], t1[:, 0:n], MAGIC, MAGIC, ALU.add, ALU.subtract, reads=["ct1"], writes=["ct2"])
                kb.tt(t1[:, 0:n], t1[:, 0:n], t2[:, 0:n], ALU.subtract, reads=["ct1", "ct2"], writes=["ct1"])
                kb.act(t2[:, 0:n], t1[:, 0:n], AF.Sin, scale=TWO_PI, reads=["ct1"], writes=["ct2"])
                kb.act(t3[:, 0:n], t1[:, 0:n], AF.Sin, scale=math.pi, reads=["ct1"], writes=["ct3"])
                kb.tt(t3[:, 0:n], t3[:, 0:n], t3[:, 0:n], ALU.mult, reads=["ct3"], writes=["ct3"])
                kb.ts(t3[:, 0:n], t3[:, 0:n], -2.0, 1.0, ALU.mult, ALU.add, reads=["ct3"], writes=["ct3"])
                kb.tt(w1r[:], t0[:, 0:n], t3[:, 0:n], ALU.mult, reads=["ct0", "ct3"], writes=["w1r"])
                kb.tt(w1i[:], t0[:, 0:n], t2[:, 0:n], ALU.mult, reads=["ct0", "ct2"], writes=["w1i"])
            cpow1()
            g0 = t32("g0", [128, 64]); g1 = t32("g1", [128, 64]); g2 = t32("g2", [128, 64])
            kb.ts(w1r[:], w1r[:], -1.0, None, ALU.add, reads=["w1r"], writes=["w1r"])
            kb.tt(g0[:], ar[:], ar[:], ALU.mult, reads=["ar"], writes=["g0"])
            kb.tt(g1[:], AimT[:], AimT[:], ALU.mult, reads=[AimT.name], writes=["g1"])
            kb.tt(g0[:], g0[:], g1[:], ALU.add, reads=["g0", "g1"], writes=["g0"])
            kb.recip(g0[:], g0[:], reads=["g0"], writes=["g0"])
            kb.tt(g1[:], w1r[:], ar[:], ALU.mult, reads=["w1r", "ar"], writes=["g1"])
            kb.tt(g2[:], w1i[:], AimT[:], ALU.mult, reads=["w1i", AimT.name], writes=["g2"])
            kb.tt(g1[:], g1[:], g2[:], ALU.add, reads=["g1", "g2"], writes=["g1"])
            kb.tt(gr[:], g1[:], g0[:], ALU.mult, reads=["g1", "g0"], writes=["gr"])
            kb.tt(g1[:], w1i[:], ar[:], ALU.mult, reads=["w1i", "ar"], writes=["g1"])
            kb.tt(g2[:], w1r[:], AimT[:], ALU.mult, reads=["w1r", AimT.name], writes=["g2"])
            kb.tt(g1[:], g1[:], g2[:], ALU.subtract, reads=["g1", "g2"], writes=["g1"])
            kb.tt(gi_[:], g1[:], g0[:], ALU.mult, reads=["g1", "g0"], writes=["gi_"])

            def coef_tiles(name):
                return [t32(f"{name}{i}", [128, 512]) for i in range(2)]
            SH = [128, 2, 32, 8]
            def V4(t):
                return t[:].rearrange("p (a b c) -> p a b c", a=2, b=32)
            def pwview(ti):
                return PW[:, ti, :, :].unsqueeze(2).broadcast_to(SH)
            grb = gr[:].rearrange("p (a b) -> p a b", a=2).unsqueeze(3).broadcast_to(SH)
            gib = gi_[:].rearrange("p (a b) -> p a b", a=2).unsqueeze(3).broadcast_to(SH)
            cre = t32("cre", [128, 512]); cim = t32("cim", [128, 512])
            ere = t32("ere", [128, 512]); eim = t32("eim", [128, 512])

            def pform(p1, p2, re_t, im_t, tag):
                kb.cp(p1[0:64, :], re_t[0:64, :], reads=[tag + "_re"], writes=[p1.name])
                kb.ts(p1[64:128, :], im_t[64:128, :], -1.0, None, ALU.mult, reads=[tag + "_im"], writes=[p1.name])
                kb.ts(p2[0:64, :], im_t[0:64, :], -1.0, None, ALU.mult, reads=[tag + "_im"], writes=[p2.name])
                kb.ts(p2[64:128, :], re_t[64:128, :], -1.0, None, ALU.mult, reads=[tag + "_re"], writes=[p2.name])

            def qform(q1, q2, re_t, im_t, tag):
                kb.cp(q1[0:64, :], re_t[0:64, :], reads=[tag + "_re"], writes=[q1.name])
                kb.cp(q1[64:128, :], im_t[64:128, :], reads=[tag + "_im"], writes=[q1.name])
                kb.ts(q2[0:64, :], im_t[0:64, :], -1.0, None, ALU.mult, reads=[tag + "_im"], writes=[q2.name])
                kb.cp(q2[64:128, :], re_t[64:128, :], reads=[tag + "_re"], writes=[q2.name])

            def times_gamma(tag):
                t0, t1, _, _ = ctmp
                kb.tt(V4(t0), V4(cre), grb, ALU.mult, reads=[tag + "_re", "gr"], writes=["ct0"])
                kb.tt(V4(t1), V4(cim), gib, ALU.mult, reads=[tag + "_im", "gi_"], writes=["ct1"])
                kb.tt(ere[:], t0[:], t1[:], ALU.subtract, reads=["ct0", "ct1"], writes=[tag + "g_re"])
                kb.tt(V4(t0), V4(cre), gib, ALU.mult, reads=[tag + "_re", "gi_"], writes=["ct0"])
                kb.tt(V4(t1), V4(cim), grb, ALU.mult, reads=[tag + "_im", "gr"], writes=["ct1"])
                kb.tt(eim[:], t0[:], t1[:], ALU.add, reads=["ct0", "ct1"], writes=[tag + "g_im"])

            cpow(cre[:], cim[:], pwview(0), 512, "cs")
            pform(P1c, P2c, cre, cim, "cs")
            cpow(cre[:], cim[:], pwview(1), 512, "bs")
            times_gamma("bs")
            qform(Q1b, Q2b, ere, eim, "bsg")
            cpow(cre[:], cim[:], pwview(2), 512, "gg")
            times_gamma("gg")
            pform(P1g, P2g, ere, eim, "ggg")
            cpow(cre[:], cim[:], pwview(3), 512, "aa")
            kb.cp(Aar[:], cre[:], reads=["aa_re"], writes=[Aar.name])
            kb.ts(Asw[0:64, :], cim[0:64, :], -1.0, None, ALU.mult, reads=["aa_im"], writes=[Asw.name])
            kb.cp(Asw[64:128, :], cim[64:128, :], reads=["aa_im"], writes=[Asw.name])

            P.barrier()
            sb0.close()
            Cs_t = sb(sbb, "Cs_t", [128, 2, 8, 128], BF16)
            Bs_t = sb(sbb, "Bs_t", [128, 2, 8, 128], BF16)
            M0_t = sb(sbb, "M0_t", [128, 8, 128], BF16)
            psR = ps(sbb, "psR", [128, 512], F32)
            psE = ps(sbb, "psE", [128, 512], F32)
            psM = ps(sbb, "psM", [128, 512], F32)
            psT = ps(sbb, "psT", [128, 1024], BF16)
            psS = [ps(sbb, f"psS{i}", [128, 512], F32) for i in range(2)]
            psY = [ps(sbb, f"psY{i}", [128, 512], F32) for i in range(2)]
            batches = cfg.get("batches", [0, 1, 2, 3])
            U8s = [sb(sbb, f"U8_{i}", [128, 8, J], BF16) for i in range(2)]
            Y8s = [sb(sbb, f"Y8_{i}", [128, 8, 512], BF16) for i in range(2)]

            def load_u8(bi):
                ct_ = batches[bi]
                for a in range(8):
                    kb.dma(U8s[bi % 2][16 * a:16 * a + 16, :, :],
                           Ud[ct_ * 128:(ct_ + 1) * 128, a, :].rearrange("(g h) j -> h g j", h=16),
                           writes=[f"U8_{bi % 2}"], defer=True)
            load_u8(0)
            for bi, ct in enumerate(batches):
                gs = ct * 8
                U8 = U8s[bi % 2]; Y8 = Y8s[bi % 2]
                u8k = f"U8_{bi % 2}"; y8k = f"Y8_{bi % 2}"
                sT = contextlib.ExitStack()
                CrT = sb(sT, f"CrT{ct}", [128, 2, 8, 16], F32); CiT = sb(sT, f"CiT{ct}", [128, 2, 8, 16], F32)
                BrT = sb(sT, f"BrT{ct}", [128, 2, 8, 16], F32); BiT = sb(sT, f"BiT{ct}", [128, 2, 8, 16], F32)
                Bm2 = sb(sT, f"Bm2{ct}", [128, 2, 8, 16], F32)
                Gbuf = sb(sT, f"Gbuf{ct}", [128, 2, 8, 240], BF16)
                Bpad = sb(sT, f"Bpad{ct}", [128, 2, 8, 240], BF16)
                BsT = sb(sT, f"BsT{ct}", [128, 2, 8, 128], BF16)
                tA = sb(sT, f"tA{ct}", [128, 1024], F32); tB = sb(sT, f"tB{ct}", [128, 1024], F32)
                for tl, nm in ((CrT, "CrT2"), (CiT, "CiT2"), (BrT, "BrT2"), (BiT, "BiT2"), (Bm2, "Bmat2")):
                    for d in range(2):
                        kb.dma(tl[:, d, :, :], D[nm][:, d * 32 + gs:d * 32 + gs + 8, :], writes=[tl.name])
                kb.memset(Gbuf[:], 0.0, writes=["Gbuf"])
                kb.memset(Bpad[:], 0.0, writes=["Bpad"])
                for d in range(2):
                    Crb = CrT[:, d, :, :]; Cib = CiT[:, d, :, :]
                    Brb = BrT[:, d, :, :]; Bib = BiT[:, d, :, :]
                    sh4 = [128, 8, 8, 16]
                    def cf(t):
                        return V4(t)[:, d, gs:gs + 8, :].unsqueeze(3).broadcast_to(sh4)
                    def bc(v):
                        return v.unsqueeze(2).broadcast_to(sh4)
                    tAv = tA[:].rearrange("p (g k o) -> p g k o", g=8, k=8)
                    tBv = tB[:].rearrange("p (g k o) -> p g k o", g=8, k=8)
                    kb.tt(tAv, bc(Crb), cf(P1c), ALU.mult, reads=[CrT.name, P1c.name], writes=["tA"])
                    kb.tt(tBv, bc(Cib), cf(P2c), ALU.mult, reads=[CiT.name, P2c.name], writes=["tB"])
                    kb.tt(Cs_t[:, d, :, :].rearrange("p g x -> p (g x)"), tA[:], tB[:], ALU.add,
                          reads=["tA", "tB"], writes=["Cs_t"])
                    kb.tt(tAv, bc(Crb), cf(P1g), ALU.mult, reads=[CrT.name, P1g.name], writes=["tA"])
                    kb.tt(tBv, bc(Cib), cf(P2g), ALU.mult, reads=[CiT.name, P2g.name], writes=["tB"])
                    b0 = 7 * 16 if d == 0 else 0
                    kb.tt(Gbuf[:, d, :, b0:b0 + 128], tA[:].rearrange("p (g x) -> p g x", g=8),
                          tB[:].rearrange("p (g x) -> p g x", g=8), ALU.add, reads=["tA", "tB"], writes=["Gbuf"])
                    kb.cp(Bpad[:, d, :, 112:128], Bm2[:, d, :, :], reads=[Bm2.name], writes=["Bpad"])
                    kb.tt(tAv, bc(Brb), cf(Q1b), ALU.mult, reads=[BrT.name, Q1b.name], writes=["tA"])
                    kb.tt(tBv, bc(Bib), cf(Q2b), ALU.mult, reads=[BiT.name, Q2b.name], writes=["tB"])
                    kb.tt(BsT[:, d, :, :].rearrange("p g x -> p (g x)"), tA[:], tB[:], ALU.add,
                          reads=["tA", "tB"], writes=["BsT"])
                    for gi in range(8):
                        kb.tr(psT[:, gi * 128:(gi + 1) * 128], BsT[:, d, gi, :], ident_b[:], reads=["BsT", "ident_b"],
                              writes=["psT"])
                    kb.cp(Bs_t[:, d, :, :].rearrange("p g x -> p (g x)"), psT[:], reads=["psT"], writes=["Bs_t"],
                          eng="act")
                for gi in range(8):
                    n_mm = 0
                    for d in range(2):
                        for a in range(8):
                            w0 = (7 - a) * 16
                            kb.mm(psM[:, 0:128], Bpad[:, d, gi, w0:w0 + 128], Gbuf[:, d, gi, w0:w0 + 128],
                                  start=(n_mm == 0), stop=(n_mm == 15), reads=["Bpad", "Gbuf"], writes=["psM"])
                            n_mm += 1
                    kb.stt(M0_t[:, gi, :], ident_f[:], Dv[:, gs + gi:gs + gi + 1], psM[:, 0:128], ALU.mult, ALU.add,
                           reads=["ident_f", Dv.name, "psM"], writes=["M0_t"])
                P.barrier(include_deferred=False)
                sT.close()
                sR = contextlib.ExitStack()
                XL = sb(sR, f"XL{ct}", [128, 8, NL], F32); XS = sb(sR, f"XS{ct}", [128, 8, NS], F32)
                XLb = sb(sR, f"XLb{ct}", [128, 8, 512], BF16); XSb = sb(sR, f"XSb{ct}", [128, 8, 512], BF16)
                EL = sb(sR, f"EL{ct}", [128, 8, 41], F32); ES = sb(sR, f"ES{ct}", [128, 8, 21], F32)
                DL = sb(sR, f"DL{ct}", [128, 8, 40], F32); DS = sb(sR, f"DS{ct}", [128, 8, 20], F32)
                DL2 = sb(sR, f"DL2{ct}", [128, 8, 40], F32); DS2 = sb(sR, f"DS2{ct}", [128, 8, 20], F32)
                tL = sb(sR, f"tL{ct}", [128, 8, 40], F32); uL = sb(sR, f"uL{ct}", [128, 8, 40], F32)
                tS = sb(sR, f"tS{ct}", [128, 8, 20], F32); uS = sb(sR, f"uS{ct}", [128, 8, 20], F32)
                tE = sb(sR, f"tE{ct}", [128, 16], F32); uE = sb(sR, f"uE{ct}", [128, 16], F32)
                kb.memset(XL[:, :, 0:14], 0.0, writes=["XL"])
                kb.memset(XS[:, :, 514:NS], 0.0, writes=["XS"])
                ev = 0
                for gi in range(8):
                    for (d, src0, n, dst) in ((0, 0, 512, XL[:, gi, 16:528]), (0, 512, 512, XL[:, gi, 528:1040]),
                                              (1, 512, 512, XS[:, gi, 0:512])):
                        pp = psS[ev % 2]
                        kb.mm(pp[:, 0:n], Bs_t[:, d, gi, :], U8[:, gi, src0:src0 + n], reads=["Bs_t", u8k],
                              writes=[pp.name])
                        kb.cp(dst, pp[:, 0:n], reads=[pp.name], writes=["XL" if d == 0 else "XS"],
                              eng=("act" if ev % 2 else "dve"))
                        ev += 1
                    for (d, dst, fc) in ((0, XL[:, gi, 14:16], 0), (1, XS[:, gi, 512:514], 1)):
                        pp = psS[ev % 2]
                        kb.mm(pp[:, 0:2], Bs_t[:, d, gi, :], U8[:, gi, 1024:1026], reads=["Bs_t", u8k], writes=[pp.name])
                        kb.ts(dst, pp[:, 0:2], flg[:, fc:fc + 1], None, ALU.mult, reads=[pp.name, flg.name],
                              writes=["XL" if d == 0 else "XS"])
                        ev += 1
                if bi + 1 < len(batches):
                    load_u8(bi + 1)
                def coef(t, d, kk, n):
                    return V4(t)[:, d, gs:gs + 8, kk].unsqueeze(2).broadcast_to([128, 8, n])
                arL = coef(Aar, 0, 0, 40); swL = coef(Asw, 0, 0, 40)
                arS = coef(Aar, 1, 0, 20); swS = coef(Asw, 1, 0, 20)
                cfk = [Aar.name, Asw.name]
                psRL = psR[:, 0:320].rearrange("p (g s) -> p g s", g=8)
                psRS = psR[:, 320:480].rearrange("p (g s) -> p g s", g=8)
                for it in range(1, SEGL):
                    isn = SEGL - 1 - it
                    XLp = XL[:, :, it - 1::SEGL]; XLc = XL[:, :, it::SEGL]
                    XSp = XS[:, :, isn + 1::SEGL]; XSc = XS[:, :, isn::SEGL]
                    kb.mm(psRL, swp[:], XLp, reads=[swp.name, "XL"], writes=["psR"])
                    kb.mm(psRS, swp[:], XSp, reads=[swp.name, "XS"], writes=["psR"])
                    kb.tt(tL[:], XLp, arL, ALU.mult, reads=["XL"] + cfk, writes=["tL"], eng="pool")
                    kb.tt(tS[:], XSp, arS, ALU.mult, reads=["XS"] + cfk, writes=["tS"], eng="pool")
                    kb.tt(uL[:], psRL, swL, ALU.mult, reads=["psR"] + cfk, writes=["uL"])
                    kb.tt(uS[:], psRS, swS, ALU.mult, reads=["psR"] + cfk, writes=["uS"])
                    kb.tt(XLc, XLc, tL[:], ALU.add, reads=["XL", "tL"], writes=["XL"], eng="pool")
                    kb.tt(XSc, XSc, tS[:], ALU.add, reads=["XS", "tS"], writes=["XS"], eng="pool")
                    kb.tt(XLc, XLc, uL[:], ALU.add, reads=["XL", "uL"], writes=["XL"])
                    kb.tt(XSc, XSc, uS[:], ALU.add, reads=["XS", "uS"], writes=["XS"])
                a26L = V4(Aar)[:, 0, gs:gs + 8, 1]; s26L = V4(Asw)[:, 0, gs:gs + 8, 1]
                a26S = V4(Aar)[:, 1, gs:gs + 8, 1]; s26S = V4(Asw)[:, 1, gs:gs + 8, 1]
                kb.memset(EL[:, :, 0:1], 0.0, writes=["EL"])
                kb.memset(ES[:, :, 20:21], 0.0, writes=["ES"])
                for n_ in range(40):
                    kb.mm(psE[:, 0:8], swp[:], EL[:, :, n_], reads=[swp.name, "EL"], writes=["psE"])
                    kb.tt(tE[:, 0:8], EL[:, :, n_], a26L, ALU.mult, reads=["EL"] + cfk, writes=["tEL"], eng="pool")
                    kb.tt(uE[:, 0:8], psE[:, 0:8], s26L, ALU.mult, reads=["psE"] + cfk, writes=["uEL"])
                    kb.tt(tE[:, 0:8], tE[:, 0:8], XL[:, :, SEGL * n_ + SEGL - 1], ALU.add, reads=["tEL", "XL"],
                          writes=["tEL"], eng="pool")
                    kb.tt(EL[:, :, n_ + 1], tE[:, 0:8], uE[:, 0:8], ALU.add, reads=["tEL", "uEL", "EL"], writes=["EL"])
                    if n_ < 20:
                        s_ = 19 - n_
                        kb.mm(psE[:, 8:16], swp[:], ES[:, :, s_ + 1], reads=[swp.name, "ES"], writes=["psE"])
                        kb.tt(tE[:, 8:16], ES[:, :, s_ + 1], a26S, ALU.mult, reads=["ES"] + cfk, writes=["tES"], eng="pool")
                        kb.tt(uE[:, 8:16], psE[:, 8:16], s26S, ALU.mult, reads=["psE"] + cfk, writes=["uES"])
                        kb.tt(tE[:, 8:16], tE[:, 8:16], XS[:, :, SEGL * s_], ALU.add, reads=["tES", "XS"],
                              writes=["tES"], eng="pool")
                        kb.tt(ES[:, :, s_], tE[:, 8:16], uE[:, 8:16], ALU.add, reads=["tES", "uES", "ES"], writes=["ES"])
                Dl = [DL, DL2]; Ds = [DS, DS2]
                for it in range(SEGL):
                    isn = SEGL - 1 - it
                    pl = EL[:, :, 0:40] if it == 0 else Dl[(it - 1) % 2][:]
                    psv = ES[:, :, 1:21] if it == 0 else Ds[(it - 1) % 2][:]
                    dl = Dl[it % 2]; ds_ = Ds[it % 2]
                    dk = ["EL", "ES", "DLa", "DLb", "DSa", "DSb"]
                    kb.mm(psRL, swp[:], pl, reads=[swp.name] + dk, writes=["psR"])
                    kb.mm(psRS, swp[:], psv, reads=[swp.name] + dk, writes=["psR"])
                    kb.tt(tL[:], pl, arL, ALU.mult, reads=dk + cfk, writes=["tL"], eng="pool")
                    kb.tt(tS[:], psv, arS, ALU.mult, reads=dk + cfk, writes=["tS"], eng="pool")
                    kb.tt(dl[:], psRL, swL, ALU.mult, reads=["psR"] + cfk, writes=["DLa" if it % 2 == 0 else "DLb"])
                    kb.tt(ds_[:], psRS, swS, ALU.mult, reads=["psR"] + cfk, writes=["DSa" if it % 2 == 0 else "DSb"])
                    kb.tt(dl[:], dl[:], tL[:], ALU.add, reads=["tL", "DLa", "DLb"], writes=["DLa" if it % 2 == 0 else "DLb"])
                    kb.tt(ds_[:], ds_[:], tS[:], ALU.add, reads=["tS", "DSa", "DSb"], writes=["DSa" if it % 2 == 0 else "DSb"])
                    kb.tt(XL[:, :, it::SEGL], XL[:, :, it::SEGL], dl[:], ALU.add, reads=["XL", "DLa", "DLb"],
                          writes=["XL"], eng="pool")
                    kb.tt(XS[:, :, isn::SEGL], XS[:, :, isn::SEGL], ds_[:], ALU.add, reads=["XS", "DSa", "DSb"],
                          writes=["XS"], eng="pool")
                kb.cp(XLb[:], XL[:, :, 527:1039], reads=["XL"], writes=["XLb"])
                kb.cp(XSb[:], XS[:, :, 1:513], reads=["XS"], writes=["XSb"], eng="pool")
                for gi in range(8):
                    g = gs + gi
                    pp = psY[gi % 2]
                    kb.mm(pp[:, 0:512], M0_t[:, gi, :], U8[:, gi, 512:1024], start=True, stop=False,
                          reads=["M0_t", u8k], writes=[pp.name])
                    kb.mm(pp[:, 0:512], Cs_t[:, 0, gi, :], XLb[:, gi, :], start=False, stop=False,
                          reads=["Cs_t", "XLb"], writes=[pp.name])
                    kb.mm(pp[:, 0:512], Cs_t[:, 1, gi, :], XSb[:, gi, :], start=False, stop=True,
                          reads=["Cs_t", "XSb"], writes=[pp.name])
                    kb.cp(Y8[:, gi, :], pp[:, 0:512], reads=[pp.name], writes=[y8k], eng=("act" if gi % 2 else "dve"))

                for a in range(8):
                    kb.dma(Yd[ct * 128:(ct + 1) * 128, a, :].rearrange("(g h) j -> h g j", h=16),
                           Y8[16 * a:16 * a + 16, :, :], reads=[y8k], writes=["Yd"], defer=True)
                P.barrier(include_deferred=False)
                sR.close()
        P.barrier()
        if stage == "b":
            with contextlib.ExitStack() as sd:
                d0 = sb(sd, "dbgb", [128, 4096], BF16)
                d1 = sb(sd, "dbgb1", [128, 8192], F32)
                kb.dma(d0[:], Yd[0:128, :, :].rearrange("c a j -> c (a j)"), writes=["dbgb"])
                kb.memset(d1[:], 0.0, writes=["dbgb1"])
                kb.cp(d1[:, 0:4096], d0[:], reads=["dbgb", "dbgb1"], writes=["dbgb2"])
                kb.dma(dbg, d1[:], reads=["dbgb2"])
            P.emit()
            return nc

        S3 = S1.enter_context(contextlib.ExitStack())
        ssm_tm = sb(S3, "ssm_tm", [128, 32, 512], BF16)
        with contextlib.ExitStack() as sb2:
            w_glu_bf = sb(sb2, "w_glu_bf", [128, 4, 1024], BF16)
            stgG = sb(sb2, "stgG", [128, 1024], F32)
            for k in range(4):
                load_w(stgG, w_glu_bf[:, k, :], D["w_glu"][k * 128:(k + 1) * 128, :], 1024, None, None, "w_glu_bf")
            gyT = sb(sb2, "gyT", [128, 4, HALF], BF16)
            yd = [sb(sb2, f"yd{i}", [128, 8, 512], BF16) for i in range(2)]
            gt = [sb(sb2, f"gt{i}", [128, 512], F32) for i in range(2)]
            gs_ = [sb(sb2, f"gs{i}", [128, 512], F32) for i in range(2)]
            sig = [sb(sb2, f"sig{i}", [128, 512], F32) for i in range(2)]
            pz = [ps(sb2, f"pz{i}", [128, 512], F32) for i in range(4)]
            e2 = 0
            for ct in range(4):
                kb.dma(yd[ct % 2][:], Yd[ct * 128:(ct + 1) * 128, :, :], writes=[f"yd{ct % 2}"])
                for a in range(8):
                    y = yd[ct % 2][:, a, :]
                    i2 = e2 % 2
                    kb.tt(gt[i2][:], y, y, ALU.mult, reads=[f"yd{ct % 2}"], writes=[f"gt{i2}"])
                    kb.ts(gt[i2][:], gt[i2][:], 0.044715, 1.0, ALU.mult, ALU.add, reads=[f"gt{i2}"], writes=[f"gt{i2}"])
                    kb.tt(gt[i2][:], gt[i2][:], y, ALU.mult, reads=[f"gt{i2}", f"yd{ct % 2}"], writes=[f"gt{i2}"])
                    kb.act(gs_[i2][:], gt[i2][:], AF.Sigmoid, scale=1.5957691216057308, reads=[f"gt{i2}"],
                           writes=[f"gs{i2}"])
                    kb.tt(gyT[:, ct, a::8], gs_[i2][:], y, ALU.mult, reads=[f"gs{i2}", f"yd{ct % 2}"], writes=["gyT"],
                          eng="pool")
                    e2 += 1
            for tq in range(cfg.get("ntq", 32)):
                pa = pz[(tq % 2) * 2]; pb_ = pz[(tq % 2) * 2 + 1]
                for c, pp in ((0, pa), (1, pb_)):
                    for k in range(4):
                        kb.mm(pp[:, 0:512], gyT[:, k, tq * 128:(tq + 1) * 128], w_glu_bf[:, k, c * 512:(c + 1) * 512],
                              start=(k == 0), stop=(k == 3), reads=["gyT", "w_glu_bf"], writes=[pp.name])
                i2 = tq % 2
                kb.act(sig[i2][:], pb_[:, 0:512], AF.Sigmoid, reads=[pb_.name], writes=[f"sig{i2}"])
                kb.tt(ssm_tm[:, tq, :], pa[:, 0:512], sig[i2][:], ALU.mult, reads=[pa.name, f"sig{i2}"],
                      writes=["ssm_tm"])
        P.barrier()

        with contextlib.ExitStack() as sd1:
            w_out_bf = sb(sd1, "w_out_bf", [128, 8, 1024], BF16)
            stgD = sb(sd1, "stgD", [128, 1024], F32)
            for k in range(8):
                load_w(stgD, w_out_bf[:, k, :], D["w_out"][k * 128:(k + 1) * 128, :], 1024, gvec[:, 13 + k:14 + k],
                       "gvec_gmix", "w_out_bf")
            gpost_t = sb(sd1, "gpost_t", [128, 1024], F32)
            kb.dma(gpost_t[:], D["gpost"], writes=["gpost_t"])
            xt2 = [sb(sd1, f"xt2_{i}", [128, 1024], F32) for i in range(2)]
            junkD = sb(sd1, "junkD", [128, 1024], BF16)
            mixn = [sb(sd1, f"mixn{i}", [128, 1024], BF16) for i in range(2)]
            mixT = [sb(sd1, f"mixT{i}", [128, 8, 128], BF16) for i in range(2)]
            t1 = [sb(sd1, f"t1_{i}", [128, 1024], F32) for i in range(2)]
            h1 = [sb(sd1, f"h1_{i}", [128, 1024], F32) for i in range(2)]
            hn = [sb(sd1, f"hn{i}", [128, 1024], BF16) for i in range(2)]
            hnT = [sb(sd1, f"hnT{i}", [128, 8, 128], BF16) for i in range(2)]
            stt_ = {nm: sb(sd1, "d1_" + nm, [128, 32], F32) for nm in
                    ("sa", "sat", "sar", "ss", "sst", "ssr", "so", "so2", "sot", "sor", "sm", "smt", "smr")}
            pTd = ps(sd1, "pTd", [128, 1024], BF16)
            pTe = ps(sd1, "pTe", [128, 1024], BF16)
            pmo = [[ps(sd1, f"pmo{i}{c}", [128, 512], F32) for c in range(2)] for i in range(2)]
            ntq_ = cfg.get("ntq", 32)

            def d1_A(tq):
                i2 = tq % 2
                c1 = slice(tq, tq + 1)
                kb.dma(xt2[i2][:], D["xm"][(32 + tq) * 128:(33 + tq) * 128, :], writes=[f"xt2_{i2}"])
                kb.act(junkD[:, 0:512], attn_tm[:, tq, :], AF.Square, writes=["junkD", f"sa{tq}"], accum=stt_["sa"][:, c1])
                kb.rstd(stt_["sar"][:, c1], stt_["sa"][:, c1], stt_["sat"][:, c1], mhalf[:, 0:1], 1.0 / 512,
                        reads=[f"sa{tq}"], writes=[f"sar{tq}"], tmpkey=f"sat{tq}")
                kb.act(junkD[:, 0:512], ssm_tm[:, tq, :], AF.Square, writes=["junkD", f"ss{tq}"], accum=stt_["ss"][:, c1])
                kb.rstd(stt_["ssr"][:, c1], stt_["ss"][:, c1], stt_["sst"][:, c1], mhalf[:, 0:1], 1.0 / 512,
                        reads=[f"ss{tq}"], writes=[f"ssr{tq}"], tmpkey=f"sst{tq}")
                kb.ts(mixn[i2][:, 0:512], attn_tm[:, tq, :], stt_["sar"][:, c1], None, ALU.mult, reads=[f"sar{tq}"],
                      writes=[f"mixn{i2}"])
                kb.ts(mixn[i2][:, 512:1024], ssm_tm[:, tq, :], stt_["ssr"][:, c1], None, ALU.mult, reads=[f"ssr{tq}"],
                      writes=[f"mixn{i2}"], eng="pool")
                for k in range(8):
                    kb.tr(pTd[:, k * 128:(k + 1) * 128], mixn[i2][:, k * 128:(k + 1) * 128], ident_b[:],
                          reads=[f"mixn{i2}", "ident_b"], writes=["pTd"])
                kb.cp(mixT[i2][:].rearrange("p k t -> p (k t)"), pTd[:], reads=["pTd"], writes=[f"mixT{i2}"], eng="act")
                for c in range(2):
                    for k in range(8):
                        kb.mm(pmo[i2][c][:, 0:512], mixT[i2][:, k, :], w_out_bf[:, k, c * 512:(c + 1) * 512],
                              start=(k == 0), stop=(k == 7), reads=[f"mixT{i2}", "w_out_bf"], writes=[f"pmo{i2}{c}"])

            def d1_B(tq):
                i2 = tq % 2
                c1 = slice(tq, tq + 1)
                kb.act(junkD[:, 0:512], pmo[i2][0][:, 0:512], AF.Square, reads=[f"pmo{i2}0"], writes=["junkD", f"so{tq}"],
                       accum=stt_["so"][:, c1])
                kb.act(junkD[:, 512:1024], pmo[i2][1][:, 0:512], AF.Square, reads=[f"pmo{i2}1"], writes=["junkD", f"so2{tq}"],
                       accum=stt_["so2"][:, c1])
                kb.tt(stt_["so"][:, c1], stt_["so"][:, c1], stt_["so2"][:, c1], ALU.add, reads=[f"so{tq}", f"so2{tq}"],
                      writes=[f"so{tq}"])
                kb.rstd(stt_["sor"][:, c1], stt_["so"][:, c1], stt_["sot"][:, c1], mhalf[:, 0:1], 1.0 / 1024,
                        reads=[f"so{tq}"], writes=[f"sor{tq}"], tmpkey=f"sot{tq}")
                for c in range(2):
                    kb.tt(t1[i2][:, c * 512:(c + 1) * 512], pmo[i2][c][:, 0:512], gpost_t[:, c * 512:(c + 1) * 512],
                          ALU.mult, reads=[f"pmo{i2}{c}", "gpost_t"], writes=[f"t1_{i2}"])
                kb.stt(h1[i2][:], t1[i2][:], stt_["sor"][:, c1], xt2[i2][:], ALU.mult, ALU.add,
                       reads=[f"t1_{i2}", f"sor{tq}", f"xt2_{i2}"], writes=[f"h1_{i2}"])
                kb.dma(H1s[tq * 128:(tq + 1) * 128, :], h1[i2][:], reads=[f"h1_{i2}"], writes=["H1s"])
                kb.act(junkD[:], h1[i2][:], AF.Square, reads=[f"h1_{i2}"], writes=["junkD", f"sm{tq}"],
                       accum=stt_["sm"][:, c1])
                kb.rstd(stt_["smr"][:, c1], stt_["sm"][:, c1], stt_["smt"][:, c1], mhalf[:, 0:1], 1.0 / 1024,
                        reads=[f"sm{tq}"], writes=[f"smr{tq}"], tmpkey=f"smt{tq}")
                kb.ts(hn[i2][:], h1[i2][:], stt_["smr"][:, c1], None, ALU.mult, reads=[f"h1_{i2}", f"smr{tq}"],
                      writes=[f"hn{i2}"])
                for k in range(8):
                    kb.tr(pTe[:, k * 128:(k + 1) * 128], hn[i2][:, k * 128:(k + 1) * 128], ident_b[:],
                          reads=[f"hn{i2}", "ident_b"], writes=["pTe"])
                kb.cp(hnT[i2][:].rearrange("p k t -> p (k t)"), pTe[:], reads=["pTe"], writes=[f"hnT{i2}"], eng="act")
                kb.dma(HnT[:, :, tq * 128:(tq + 1) * 128], hnT[i2][:], reads=[f"hnT{i2}"], writes=["HnT"])

            if ntq_:
                d1_A(0)
            for tq in range(ntq_):
                if tq + 1 < ntq_:
                    d1_A(tq + 1)
                d1_B(tq)
        P.barrier()
        S1.close()
        P.barrier()

        with contextlib.ExitStack() as sd2:
            wdn_bf = sb(sd2, "wdn_bf", [128, 32, 1024], BF16)
            for q4 in range(4):
                kb.dma(wdn_bf[:, q4 * 8:(q4 + 1) * 8, :],
                       wdn_s[q4 * 1024:(q4 + 1) * 1024, :].rearrange("(f p) n -> p f n", p=128), writes=["wdn_bf"])
            gpm_t = sb(sd2, "gpm_t", [128, 1024], F32)
            kb.dma(gpm_t[:], D["gpostmlp"], writes=["gpm_t"])
            wup = [sb(sd2, f"wup{i}", [128, 8, 512], BF16) for i in range(3)]
            hnTb = [sb(sd2, f"hnTb{i}", [128, 8, 512], BF16) for i in range(2)]
            hid = sb(sd2, "hid", [128, 32, 512], BF16)
            rl = [sb(sd2, f"rl{i}", [128, 512], BF16) for i in range(2)]
            h1t = [sb(sd2, f"h1t{i}", [128, 1024], F32) for i in range(2)]
            t2 = [sb(sd2, f"t2_{i}", [128, 1024], F32) for i in range(2)]
            ot = [sb(sd2, f"ot{i}", [128, 1024], F32) for i in range(2)]
            junkE = sb(sd2, "junkE", [128, 1024], BF16)
            s2 = {nm: sb(sd2, "d2_" + nm, [128, 32], F32) for nm in ("a", "b", "t", "r")}
            pu = [ps(sd2, f"pu{i}", [128, 512], F32) for i in range(2)]
            pm_ = [[ps(sd2, f"pm{i}{c}", [128, 512], F32) for c in range(2)] for i in range(2)]
            nblk = cfg.get("ntq", 32) // 4
            wjobs = [(blk, fg) for blk in range(nblk) for fg in range(8)]

            def wload(j):
                blk_, fg_ = wjobs[j]
                kb.dma(wup[j % 3][:], wup_s[:, fg_ * 512:(fg_ + 1) * 512].rearrange("(k p) n -> p k n", p=128),
                       writes=[f"wup{j % 3}"])

            def hload(blk_):
                kb.dma(hnTb[blk_ % 2][:], HnT[:, :, blk_ * 512:(blk_ + 1) * 512], writes=[f"hnTb{blk_ % 2}"])

            def h1load(tq_):
                kb.dma(h1t[tq_ % 2][:], H1s[tq_ * 128:(tq_ + 1) * 128, :], writes=[f"h1t{tq_ % 2}"])

            if nblk:
                hload(0)
                wload(0)
                wload(1)
                h1load(0)
            for blk in range(nblk):
                bb = blk % 2
                if blk + 1 < nblk:
                    hload(blk + 1)
                for fg in range(8):
                    j = blk * 8 + fg
                    r_ = j % 3
                    if j + 2 < len(wjobs):
                        wload(j + 2)
                    for fi in range(4):
                        f = fg * 4 + fi
                        pp = pu[f % 2]
                        for k in range(8):
                            kb.mm(pp[:, 0:512], wup[r_][:, k, fi * 128:(fi + 1) * 128], hnTb[bb][:, k, :],
                                  start=(k == 0), stop=(k == 7), reads=[f"wup{r_}", f"hnTb{bb}"], writes=[pp.name])
                        kb.act(rl[f % 2][:], pp[:, 0:512], AF.Relu, reads=[pp.name], writes=[f"rl{f % 2}"])
                        kb.tt(hid[:, f, :], rl[f % 2][:], rl[f % 2][:], ALU.mult, reads=[f"rl{f % 2}"], writes=["hid"])
                for t4 in range(4):
                    tq = blk * 4 + t4
                    i2 = tq % 2
                    c1 = slice(tq, tq + 1)
                    if tq + 1 < nblk * 4:
                        h1load(tq + 1)
                    for c in range(2):
                        pp = pm_[i2][c]
                        for f in range(32):
                            kb.mm(pp[:, 0:512], hid[:, f, t4 * 128:(t4 + 1) * 128], wdn_bf[:, f, c * 512:(c + 1) * 512],
                                  start=(f == 0), stop=(f == 31), reads=["hid", "wdn_bf"], writes=[pp.name])
                    kb.act(junkE[:, 0:512], pm_[i2][0][:, 0:512], AF.Square, reads=[pm_[i2][0].name],
                           writes=["junkE", f"e_a{tq}"], accum=s2["a"][:, c1])
                    kb.act(junkE[:, 512:1024], pm_[i2][1][:, 0:512], AF.Square, reads=[pm_[i2][1].name],
                           writes=["junkE", f"e_b{tq}"], accum=s2["b"][:, c1])
                    kb.tt(s2["a"][:, c1], s2["a"][:, c1], s2["b"][:, c1], ALU.add, reads=[f"e_a{tq}", f"e_b{tq}"],
                          writes=[f"e_a{tq}"])
                    kb.rstd(s2["r"][:, c1], s2["a"][:, c1], s2["t"][:, c1], mhalf[:, 0:1], 1.0 / 1024,
                            reads=[f"e_a{tq}"], writes=[f"e_r{tq}"], tmpkey=f"e_t{tq}")
                    for c in range(2):
                        kb.tt(t2[i2][:, c * 512:(c + 1) * 512], pm_[i2][c][:, 0:512], gpm_t[:, c * 512:(c + 1) * 512],
                              ALU.mult, reads=[pm_[i2][c].name, "gpm_t"], writes=[f"t2_{i2}"])
                    kb.stt(ot[i2][:], t2[i2][:], s2["r"][:, c1], h1t[i2][:], ALU.mult, ALU.add,
                           reads=[f"t2_{i2}", f"e_r{tq}", f"h1t{i2}"], writes=[f"ot{i2}"])
                    kb.dma(out[tq * 128:(tq + 1) * 128, :], ot[i2][:], reads=[f"ot{i2}"], writes=["out"])

        P.emit()
    return nc


def _core_inputs(c, inp):
    b, hf = c // 2, c % 2
    x = np.asarray(inp["x"], np.float32)[b]
    meta = np.asarray(inp["meta_tokens"], np.float32)
    pos = np.asarray(inp["positions"])[b].astype(np.int32)
    metapos = np.arange(NMETA, dtype=np.int32) - NMETA
    if hf == 0:
        x = x[::-1]; pos = pos[::-1]; meta = meta[::-1]; metapos = metapos[::-1]
    xm = np.zeros((NTOK, 1024), np.float32)
    xm[:SEQ] = x
    xm[SEQ:SEQ + NMETA] = meta
    pall = np.zeros((NTOK,), np.int32)
    pall[:SEQ] = pos
    pall[SEQ:SEQ + NMETA] = metapos
    posT = np.ascontiguousarray(pall.reshape(NT, 128).T)
    inv = (1.0 / (10000.0 ** (np.arange(0, 32, 2, dtype=np.float32) / 32))).astype(np.float32)
    d = {"xm": xm, "posT": posT, "inv16": np.ascontiguousarray(np.broadcast_to(inv, (128, 16)))}
    for k in ("w_in", "w_uq", "w_ukv", "w_glu", "w_out"):
        d[k] = np.ascontiguousarray(np.asarray(inp[k], np.float32)[0])
    d["w_up"] = np.ascontiguousarray(np.asarray(inp["w_mlp_up"], np.float32)[0])
    d["w_down"] = np.ascontiguousarray(np.asarray(inp["w_mlp_down"], np.float32)[0])

    def colmaj(v, n):
        return np.ascontiguousarray(np.asarray(v, np.float32).reshape(n, 128).T)
    d["gpre"] = colmaj(inp["g_pre_mix"][0], 8)
    d["gq"] = colmaj(inp["g_q_lat"][0], 3)
    d["gkv"] = colmaj(inp["g_kv_lat"][0], 2)
    d["gmix"] = colmaj(inp["g_mix_out"][0], 8)
    d["gpremlp"] = colmaj(inp["g_pre_mlp"][0], 8)
    d["gpost"] = np.ascontiguousarray(np.broadcast_to(np.asarray(inp["g_post_mix"], np.float32)[0], (128, 1024)))
    d["gpostmlp"] = np.ascontiguousarray(np.broadcast_to(np.asarray(inp["g_post_mlp"], np.float32)[0], (128, 1024)))
    dsel = [0, 1] if hf == 1 else [1, 0]

    def dup(a):
        a = np.asarray(a, np.float32)[0][dsel]
        t_ = a.reshape(64, 64).T
        return np.ascontiguousarray(np.concatenate([t_, t_], 0))
    d["AreT"] = dup(inp["ssm_A_re"]); d["AimT"] = dup(inp["ssm_A_im"])
    ldt = np.asarray(inp["ssm_log_dt"], np.float32)[0][dsel].reshape(64)
    d["ldtB"] = np.ascontiguousarray(np.broadcast_to(ldt, (128, 64)))
    Br = np.asarray(inp["ssm_B_re"], np.float32)[0][dsel]
    Bi = np.asarray(inp["ssm_B_im"], np.float32)[0][dsel]
    Cr = np.asarray(inp["ssm_C_re"], np.float32)[0][dsel]
    Ci = np.asarray(inp["ssm_C_im"], np.float32)[0][dsel]

    def nmaj(a):
        return a.reshape(64, 64, 16).transpose(1, 0, 2)
    BrN, BiN = nmaj(Br), nmaj(Bi)
    CrN, CiN = nmaj(Cr.transpose(0, 1, 3, 2)), nmaj(Ci.transpose(0, 1, 3, 2))
    d["BrT2"] = np.ascontiguousarray(np.concatenate([BrN, BrN], 0))
    d["BiT2"] = np.ascontiguousarray(np.concatenate([BiN, BiN], 0))
    d["Bmat2"] = np.ascontiguousarray(np.concatenate([BrN, BiN], 0))
    d["CrT2"] = np.ascontiguousarray(np.concatenate([CrN, CrN], 0))
    d["CiT2"] = np.ascontiguousarray(np.concatenate([CiN, CiN], 0))
    Dv = np.asarray(inp["ssm_D"], np.float32)[0].reshape(32, 16)
    d["Dvec"] = np.ascontiguousarray(np.tile(Dv.T, (8, 1)))
    fl = np.array([1.0, 0.0] if hf == 1 else [0.0, 1.0], np.float32)
    d["flags"] = np.ascontiguousarray(np.broadcast_to(fl, (128, 2)))
    k8 = np.arange(8, dtype=np.float32)
    pw = np.zeros((4, 2, 8), np.float32)
    pw[0, 0] = k8 + 1; pw[0, 1] = 8 - k8
    pw[1, 0] = 7 - k8; pw[1, 1] = k8
    pw[2, 0] = k8; pw[2, 1] = 7 - k8
    pw[3, :, 0] = 8.0; pw[3, :, 1] = 8.0 * SEGL
    d["PW"] = np.ascontiguousarray(np.broadcast_to(pw, (128, 4, 2, 8)))
    d["ident"] = np.eye(128, dtype=np.float32)
    sw = np.zeros((128, 128), np.float32)
    for k in range(128):
        sw[k, (k + 64) % 128] = 1.0
    d["swap"] = sw
    return d


_NC_CACHE = {}


def kernel(**inputs):
    if "full" not in _NC_CACHE:
        _NC_CACHE["full"] = build_program("full")
    nc = _NC_CACHE["full"]
    in_maps = [_core_inputs(c, inputs) for c in range(8)]
    res = run_bass_kernel_spmd(nc, in_maps, core_ids=list(range(8)))
    outp = np.zeros((4, SEQ, 1024), np.float32)
    for c in range(8):
        b, hf = c // 2, c % 2
        o = np.asarray(res.results[c]["out"], np.float32)
        if hf == 1:
            outp[b, HALF:] = o
        else:
            outp[b, :HALF] = o[::-1]
    return outp
```
